# Optimizing a Trainium2 kernel written in Bass

```python
import math
import jax
import jax.numpy as jnp
from jax import lax
import numpy as np

D_MODEL = 1024
BATCH = 4
SEQ = 8192
DEPTH = 2

GRID_W = 64
CTX_LEN = 256
MLA_HEADS = 8
MLA_Q_RANK = 256
MLA_KV_RANK = 128
MLA_NOPE = 64
MLA_ROPE = 32
MLA_V = 64
MLA_QK = MLA_NOPE + MLA_ROPE
MLA_W = MLA_HEADS * MLA_V
SWA_HEADS = 8
SWA_KV_HEADS = 2
SWA_DIM = 64
SWA_W = SWA_HEADS * SWA_DIM
WINDOW = 128
DIFF_HEADS = 4
DIFF_DIM = 64
DIFF_W = DIFF_HEADS * 2 * DIFF_DIM
D_FF = 4 * D_MODEL
N_MOD = 6
Q_BLOCK = 128
ROPE_BASE = 10000.0
EPS = 1e-6
NEG_INF = -1e30
MLA_SCALE = MLA_QK ** -0.5
SWA_SCALE = SWA_DIM ** -0.5
DIFF_SCALE = DIFF_DIM ** -0.5
IN_SPLITS = (MLA_Q_RANK, MLA_KV_RANK, MLA_ROPE,
             SWA_HEADS * SWA_DIM, SWA_KV_HEADS * SWA_DIM, SWA_KV_HEADS * SWA_DIM,
             DIFF_W, DIFF_W, DIFF_W,
             D_MODEL, D_MODEL, D_MODEL)
IN_COLS = sum(IN_SPLITS)

kernel_name = "hybrid_dit_mla_swa_diff_prefix"


def rms_norm(x, g):
    xf = x.astype(jnp.float32)
    y = xf * lax.rsqrt(jnp.mean(xf * xf, axis=-1, keepdims=True) + EPS)
    return (y * g.astype(jnp.float32)).astype(x.dtype)


def modulate(x, shift, scale):
    return x * (1.0 + scale) + shift


def axial_rope_tables(n_tokens, rot_dim):
    rows = n_tokens // GRID_W
    row = jnp.repeat(jnp.arange(rows), GRID_W).astype(jnp.float32)
    col = jnp.tile(jnp.arange(GRID_W), rows).astype(jnp.float32)
    half = rot_dim // 2
    freqs = ROPE_BASE ** (-jnp.arange(0, half, 2, dtype=jnp.float32) / half)
    ar = row[:, None] * freqs
    ac = col[:, None] * freqs
    return (jnp.cos(ar), jnp.sin(ar), jnp.cos(ac), jnp.sin(ac))


def _rotate_half(x, cos, sin):
    n = x.shape[-1] // 2
    x1, x2 = x[..., :n], x[..., n:]
    cos = cos[:, None, :].astype(x.dtype)
    sin = sin[:, None, :].astype(x.dtype)
    return jnp.concatenate([x1 * cos - x2 * sin, x2 * cos + x1 * sin], axis=-1)


def axial_rope(x, tab):
    cr, sr, cc, sc = tab
    half = x.shape[-1] // 2
    return jnp.concatenate([_rotate_half(x[..., :half], cr, sr),
                            _rotate_half(x[..., half:], cc, sc)], axis=-1)


def split_columns(proj):
    cuts = np.cumsum(IN_SPLITS)[:-1].tolist()
    return jnp.split(proj, cuts, axis=-1)


def prep_stream(proj, rope_mla, rope_hd, g_q_lora, w_uq, g_kv_lora, w_ukv,
                g_mla_q, g_mla_k, g_swa_q, g_swa_k, g_diff_q, g_diff_k):
    B, L = proj.shape[:2]
    q_lat, kv_lat, k_pe, sq, sk, sv, dq, dk, dv, ga, gb, gc = split_columns(proj)
    mq = (rms_norm(q_lat, g_q_lora) @ w_uq).reshape(B, L, MLA_HEADS, MLA_QK)
    kv = (rms_norm(kv_lat, g_kv_lora) @ w_ukv).reshape(B, L, MLA_HEADS, MLA_NOPE + MLA_V)
    k_pe = jnp.broadcast_to(k_pe[:, :, None, :], (B, L, MLA_HEADS, MLA_ROPE))
    mk = jnp.concatenate([kv[..., :MLA_NOPE], k_pe], axis=-1)
    mv = kv[..., MLA_NOPE:]
    mq = rms_norm(mq, g_mla_q)
    mk = rms_norm(mk, g_mla_k)
    sq = rms_norm(sq.reshape(B, L, SWA_HEADS, SWA_DIM), g_swa_q)
    sk = rms_norm(sk.reshape(B, L, SWA_KV_HEADS, SWA_DIM), g_swa_k)
    sv = sv.reshape(B, L, SWA_KV_HEADS, SWA_DIM)
    dq = rms_norm(dq.reshape(B, L, 2 * DIFF_HEADS, DIFF_DIM), g_diff_q)
    dk = rms_norm(dk.reshape(B, L, 2 * DIFF_HEADS, DIFF_DIM), g_diff_k)
    dv = dv.reshape(B, L, DIFF_HEADS, 2 * DIFF_DIM)
    if rope_mla is not None:
        mq = jnp.concatenate([mq[..., :MLA_NOPE], axial_rope(mq[..., MLA_NOPE:], rope_mla)], axis=-1)
        mk = jnp.concatenate([mk[..., :MLA_NOPE], axial_rope(mk[..., MLA_NOPE:], rope_mla)], axis=-1)
        sq = axial_rope(sq, rope_hd)
        sk = axial_rope(sk, rope_hd)
        dq = axial_rope(dq, rope_hd)
        dk = axial_rope(dk, rope_hd)
    dq = dq.reshape(B, L, DIFF_HEADS, 2, DIFF_DIM)
    dk = dk.reshape(B, L, DIFF_HEADS, 2, DIFF_DIM)
    return {"mla": (mq, mk, mv), "swa": (sq, sk, sv),
            "diff": (dq[..., 0, :], dq[..., 1, :], dk[..., 0, :], dk[..., 1, :], dv),
            "gates": (ga, gb, gc)}


def sweep_query_blocks(fn, qs):
    B, L = qs[0].shape[:2]
    nb = L // Q_BLOCK
    blocks = tuple(jnp.swapaxes(q.reshape((B, nb, Q_BLOCK) + q.shape[2:]), 0, 1) for q in qs)
    out = lax.map(lambda a: fn(a[0], a[1]), (blocks, jnp.arange(nb)))
    return jnp.swapaxes(out, 0, 1).reshape((B, L) + out.shape[3:])


def mla_attend(q, k, v):
    s = jnp.einsum('bqhd,bkhd->bhqk', q, k).astype(jnp.float32) * MLA_SCALE
    p = jax.nn.softmax(s, axis=-1)
    return jnp.einsum('bhqk,bkhd->bqhd', p.astype(v.dtype), v)


def diff_attend(q1, q2, k1, k2, v, lam):
    s1 = jnp.einsum('bqhd,bkhd->bhqk', q1, k1).astype(jnp.float32) * DIFF_SCALE
    s2 = jnp.einsum('bqhd,bkhd->bhqk', q2, k2).astype(jnp.float32) * DIFF_SCALE
    p = jax.nn.softmax(s1, axis=-1) - lam * jax.nn.softmax(s2, axis=-1)
    return jnp.einsum('bhqk,bkhd->bqhd', p.astype(v.dtype), v)


def gqa_sink_attend(q, k, v, sink, valid):
    B, Lq, H, d = q.shape
    kvh = k.shape[2]
    G = H // kvh
    qg = q.reshape(B, Lq, kvh, G, d)
    s = jnp.einsum('bqkgd,bjkd->bkgqj', qg, k).astype(jnp.float32) * SWA_SCALE
    if valid is not None:
        s = jnp.where(valid, s, NEG_INF)
    sink_col = jnp.broadcast_to(sink.astype(jnp.float32).reshape(kvh, G)[None, :, :, None, None],
                                s.shape[:-1] + (1,))
    p = jax.nn.softmax(jnp.concatenate([s, sink_col], axis=-1), axis=-1)[..., :-1]
    o = jnp.einsum('bkgqj,bjkd->bqkgd', p.astype(v.dtype), v)
    return o.reshape(B, Lq, H, d)


def latent_mixers(lat, ctp, sink, lam):
    mq, mk, mv = lat["mla"]
    _, cmk, cmv = ctp["mla"]
    mk_all = jnp.concatenate([cmk, mk], axis=1)
    mv_all = jnp.concatenate([cmv, mv], axis=1)
    y_mla = sweep_query_blocks(lambda qb, b: mla_attend(qb[0], mk_all, mv_all), (mq,))
    sq, sk, sv = lat["swa"]
    _, csk, csv = ctp["swa"]
    L = sq.shape[1]
    n_ctx = csk.shape[1]
    pad = ((0, 0), (Q_BLOCK, Q_BLOCK), (0, 0), (0, 0))
    kp = jnp.pad(sk, pad)
    vp = jnp.pad(sv, pad)

    def swa_block(qb, b):
        start = b * Q_BLOCK
        kw = lax.dynamic_slice_in_dim(kp, start, 3 * Q_BLOCK, axis=1)
        vw = lax.dynamic_slice_in_dim(vp, start, 3 * Q_BLOCK, axis=1)
        qpos = start + jnp.arange(Q_BLOCK)
        kpos = start - Q_BLOCK + jnp.arange(3 * Q_BLOCK)
        win = (jnp.abs(qpos[:, None] - kpos[None, :]) <= WINDOW) & (kpos[None, :] >= 0) & (kpos[None, :] < L)
        valid = jnp.concatenate([jnp.ones((Q_BLOCK, n_ctx), dtype=bool), win], axis=1)
        return gqa_sink_attend(qb[0], jnp.concatenate([csk, kw], axis=1),
                               jnp.concatenate([csv, vw], axis=1), sink, valid)

    y_swa = sweep_query_blocks(swa_block, (sq,))
    q1, q2, k1, k2, dv = lat["diff"]
    _, _, ck1, ck2, cdv = ctp["diff"]
    k1_all = jnp.concatenate([ck1, k1], axis=1)
    k2_all = jnp.concatenate([ck2, k2], axis=1)
    dv_all = jnp.concatenate([cdv, dv], axis=1)
    y_diff = sweep_query_blocks(lambda qb, b: diff_attend(qb[0], qb[1], k1_all, k2_all, dv_all, lam), (q1, q2))
    return y_mla, y_swa, y_diff


def context_mixers(ctp, sink, lam):
    mq, mk, mv = ctp["mla"]
    sq, sk, sv = ctp["swa"]
    q1, q2, k1, k2, dv = ctp["diff"]
    return (mla_attend(mq, mk, mv), gqa_sink_attend(sq, sk, sv, sink, None),
            diff_attend(q1, q2, k1, k2, dv, lam))


def merge_branches(ys, gates, g_diff_sub, lam_init, w_up_mla, w_up_swa, w_up_diff, w_o):
    y_mla, y_swa, y_diff = ys
    ga, gb, gc = gates
    B, L = y_mla.shape[:2]
    y_diff = rms_norm(y_diff, g_diff_sub) * (1.0 - lam_init)
    m = (jax.nn.sigmoid(ga) * (y_mla.reshape(B, L, MLA_W) @ w_up_mla)
         + jax.nn.sigmoid(gb) * (y_swa.reshape(B, L, SWA_W) @ w_up_swa)
         + jax.nn.sigmoid(gc) * (y_diff.reshape(B, L, DIFF_W) @ w_up_diff))
    return m @ w_o


def sq_relu_mlp(h, w_in, w_out):
    return jnp.square(jax.nn.relu(h @ w_in)) @ w_out


def setup_inputs(seed: int = 0) -> dict:
    key = jax.random.key(seed)
    ks = jax.random.split(key, 31)
    f32 = jnp.float32

    def nrm(i, shape, scale):
        return jax.random.normal(ks[i], shape, f32) * scale

    def gain(i, shape):
        return 1.0 + 0.05 * jax.random.normal(ks[i], shape, f32)

    D = D_MODEL
    return {
        "x": nrm(0, (BATCH, SEQ, D), 1.0),
        "c": nrm(1, (BATCH, D), 1.0),
        "ctx": nrm(2, (BATCH, CTX_LEN, D), 1.0),
        "c_ctx": nrm(3, (D,), 1.0),
        "w_mod": nrm(4, (DEPTH, D, N_MOD * D), 0.5 * D ** -0.5),
        "b_mod": nrm(5, (DEPTH, N_MOD * D), 0.01),
        "g_norm_attn": gain(6, (DEPTH, D)),
        "g_norm_mlp": gain(7, (DEPTH, D)),
        "w_in": nrm(8, (DEPTH, D, IN_COLS), D ** -0.5),
        "g_q_lora": gain(9, (DEPTH, MLA_Q_RANK)),
        "w_uq": nrm(10, (DEPTH, MLA_Q_RANK, MLA_HEADS * MLA_QK), MLA_Q_RANK ** -0.5),
        "g_kv_lora": gain(11, (DEPTH, MLA_KV_RANK)),
        "w_ukv": nrm(12, (DEPTH, MLA_KV_RANK, MLA_HEADS * (MLA_NOPE + MLA_V)), MLA_KV_RANK ** -0.5),
        "g_mla_q": gain(13, (DEPTH, MLA_QK)),
        "g_mla_k": gain(14, (DEPTH, MLA_QK)),
        "w_up_mla": nrm(15, (DEPTH, MLA_W, D), MLA_W ** -0.5),
        "g_swa_q": gain(16, (DEPTH, SWA_DIM)),
        "g_swa_k": gain(17, (DEPTH, SWA_DIM)),
        "swa_sink": nrm(18, (DEPTH, SWA_HEADS), 0.5),
        "w_up_swa": nrm(19, (DEPTH, SWA_W, D), SWA_W ** -0.5),
        "g_diff_q": gain(20, (DEPTH, DIFF_DIM)),
        "g_diff_k": gain(21, (DEPTH, DIFF_DIM)),
        "lambda_q1": nrm(22, (DEPTH, DIFF_DIM), 0.1),
        "lambda_k1": nrm(23, (DEPTH, DIFF_DIM), 0.1),
        "lambda_q2": nrm(24, (DEPTH, DIFF_DIM), 0.1),
        "lambda_k2": nrm(25, (DEPTH, DIFF_DIM), 0.1),
        "g_diff_sub": gain(26, (DEPTH, 2 * DIFF_DIM)),
        "w_up_diff": nrm(27, (DEPTH, DIFF_W, D), DIFF_W ** -0.5),
        "w_o": nrm(28, (DEPTH, D, D), D ** -0.5),
        "w_mlp_in": nrm(29, (DEPTH, D, D_FF), D ** -0.5),
        "w_mlp_out": nrm(30, (DEPTH, D_FF, D), D_FF ** -0.5),
    }


def reference(x, c, ctx, c_ctx, w_mod, b_mod, g_norm_attn, g_norm_mlp, w_in,
              g_q_lora, w_uq, g_kv_lora, w_ukv, g_mla_q, g_mla_k, w_up_mla,
              g_swa_q, g_swa_k, swa_sink, w_up_swa,
              g_diff_q, g_diff_k, lambda_q1, lambda_k1, lambda_q2, lambda_k2, g_diff_sub, w_up_diff,
              w_o, w_mlp_in, w_mlp_out):
    L = x.shape[1]
    rope_mla = axial_rope_tables(L, MLA_ROPE)
    rope_hd = axial_rope_tables(L, SWA_DIM)
    silu_c = jax.nn.silu(c)
    silu_cc = jax.nn.silu(c_ctx)
    cx = ctx
    for l in range(DEPTH):
        last = l == DEPTH - 1
        mod = silu_c @ w_mod[l] + b_mod[l]
        mod_c = silu_cc @ w_mod[l] + b_mod[l]
        sh1, sc1, g1, sh2, sc2, g2 = jnp.split(mod[:, None, :], N_MOD, axis=-1)
        csh1, csc1, cg1, csh2, csc2, cg2 = jnp.split(mod_c[None, None, :], N_MOD, axis=-1)
        group_params = (g_q_lora[l], w_uq[l], g_kv_lora[l], w_ukv[l], g_mla_q[l], g_mla_k[l],
                        g_swa_q[l], g_swa_k[l], g_diff_q[l], g_diff_k[l])
        h = modulate(rms_norm(x, g_norm_attn[l]), sh1, sc1)
        hc = modulate(rms_norm(cx, g_norm_attn[l]), csh1, csc1)
        lat = prep_stream(h @ w_in[l], rope_mla, rope_hd, *group_params)
        ctp = prep_stream(hc @ w_in[l], None, None, *group_params)
        lam_init = 0.8 - 0.6 * math.exp(-0.3 * l)
        lam = (jnp.exp(jnp.sum(lambda_q1[l].astype(jnp.float32) * lambda_k1[l].astype(jnp.float32)))
               - jnp.exp(jnp.sum(lambda_q2[l].astype(jnp.float32) * lambda_k2[l].astype(jnp.float32)))
               + lam_init)
        out_params = (g_diff_sub[l], lam_init, w_up_mla[l], w_up_swa[l], w_up_diff[l], w_o[l])
        y = merge_branches(latent_mixers(lat, ctp, swa_sink[l], lam), lat["gates"], *out_params)
        x = x + g1 * y
        x = x + g2 * sq_relu_mlp(modulate(rms_norm(x, g_norm_mlp[l]), sh2, sc2), w_mlp_in[l], w_mlp_out[l])
        if not last:
            yc = merge_branches(context_mixers(ctp, swa_sink[l], lam), ctp["gates"], *out_params)
            cx = cx + cg1 * yc
            cx = cx + cg2 * sq_relu_mlp(modulate(rms_norm(cx, g_norm_mlp[l]), csh2, csc2),
                                        w_mlp_in[l], w_mlp_out[l])
    return x
```

```python
import math
from contextlib import ExitStack

import numpy as np
import ml_dtypes

import concourse.bass as bass
import concourse.mybir as mybir
from concourse.bass_utils import run_bass_kernel_spmd

F32 = mybir.dt.float32
BF16 = mybir.dt.bfloat16
AF = mybir.ActivationFunctionType
ALU = mybir.AluOpType
AX = mybir.AxisListType

D = 1024
DEPTH = 2
GRID_W = 64
NCTX_T = 2
EPS = 1e-6
MLA_SCALE = 96 ** -0.5
HD_SCALE = 64 ** -0.5
IN_COLS = 5792
N_TM = 2720
NDS = 12
DEBUG = False
NLAYERS = DEPTH
DEBUG_RES = []


class Buf:
    __slots__ = ("w", "r")

    def __init__(self):
        self.w = {}
        self.r = {}


class Tile:
    def __init__(self, t):
        self.t = t
        self.b = Buf()

    def __getitem__(self, k):
        return self.t[k]


def _b(x):
    return x.b if hasattr(x, "b") else x


class Eng:
    def __init__(self, eng, sem_idx, strict=True):
        self.eng = eng
        self.sem = sem_idx
        self.n = 0
        self.waited = {}
        self.strict = strict


class DmaQ:
    def __init__(self, E, sems):
        self.E = E
        self.sems = sems
        self.cnt = [0] * len(sems)
        self.i = 0


class KB:
    def __init__(self, nc, es):
        self.nc = nc
        self.sems = []

        def mk(name):
            self.sems.append(es.enter_context(nc.semaphore(name)))
            return len(self.sems) - 1

        self.PE = Eng(nc.tensor, mk("s_pe"), strict=False)
        self.ACT = Eng(nc.scalar, mk("s_act"))
        self.DVE = Eng(nc.vector, mk("s_dve"))
        self.POOL = Eng(nc.gpsimd, mk("s_pool"))
        self.SP = Eng(nc.sync, None)
        self.q = {
            "sp": DmaQ(self.SP, [mk("d_sp%d" % i) for i in range(NDS)]),
            "pool": DmaQ(self.POOL, [mk("d_pl%d" % i) for i in range(NDS)]),
        }
        self.ninst = 0

    def _deps(self, reads, writes, partial):
        deps = {}

        def add(d):
            for s, v in d.items():
                if deps.get(s, 0) < v:
                    deps[s] = v

        for x in reads:
            add(_b(x).w)
        for x in writes:
            b = _b(x)
            add(b.w)
            add(b.r)
        for x in partial:
            add(_b(x).r)
        return deps

    def _wait(self, E, deps):
        for s, v in deps.items():
            if E.waited.get(s, 0) >= v:
                continue
            E.eng.wait_ge(self.sems[s], v)
            E.waited[s] = v

    @staticmethod
    def _mark(reads, writes, partial, s, v):
        for x in reads:
            b = _b(x)
            if b.r.get(s, 0) < v:
                b.r[s] = v
        for x in list(writes) + list(partial):
            b = _b(x)
            if b.w.get(s, 0) < v:
                b.w[s] = v

    def op(self, E, fn, reads=(), writes=(), partial=()):
        deps = self._deps(reads, writes, partial)
        if not E.strict:
            deps.pop(E.sem, None)
        self._wait(E, deps)
        ins = fn(E.eng)
        E.n += 1
        ins.then_inc(self.sems[E.sem], 1)
        self._mark(reads, writes, partial, E.sem, E.n)
        self.ninst += 1

    def dma(self, qn, out, in_, reads=(), writes=(), partial=(), slow=False):
        q = self.q[qn]
        E = q.E
        k = q.i % len(q.sems)
        q.i += 1
        s = q.sems[k]
        deps = self._deps(reads, writes, partial)
        if q.cnt[k] > 0:
            deps[s] = max(deps.get(s, 0), 16 * q.cnt[k])
        self._wait(E, deps)
        if slow:
            ins = E.eng.dma_start(out=out, in_=in_, allow_slow_non_contiguous=True)
        else:
            ins = E.eng.dma_start(out=out, in_=in_)
        q.cnt[k] += 1
        ins.then_inc(self.sems[s], 16)
        self._mark(reads, writes, partial, s, 16 * q.cnt[k])
        self.ninst += 1

    def barrier(self):
        deps = {}
        for q in self.q.values():
            for k, sm in enumerate(q.sems):
                if q.cnt[k]:
                    deps[sm] = 16 * q.cnt[k]
        for E in (self.PE, self.ACT, self.DVE, self.POOL):
            if E.n:
                deps[E.sem] = E.n
        for E in (self.PE, self.ACT, self.DVE, self.POOL, self.SP):
            self._wait(E, dict(deps))

    def finish(self):
        deps = {}
        for q in self.q.values():
            for k, s in enumerate(q.sems):
                if q.cnt[k]:
                    deps[s] = 16 * q.cnt[k]
        for E in (self.PE, self.ACT, self.DVE, self.POOL):
            if E.n:
                deps[E.sem] = E.n
        self._wait(self.SP, deps)


def build_program(NLT):
    TT = NLT + NCTX_T
    T = TT * 128
    NOWN = NLT // 2
    nc = bass.Bass("TRN2", target_bir_lowering=False)

    def din(name, shape, dt=F32):
        return Tile(nc.dram_tensor(name, list(shape), dt, kind="ExternalInput").ap())

    def dscr(name, shape, dt=BF16):
        return Tile(nc.dram_tensor(name, list(shape), dt, kind=("ExternalOutput" if DEBUG else "Internal")).ap())

    XIN = din("xin", [T, D])
    CT = din("cT", [128, 16])
    ROPE = din("rope", [T, 192])
    IDF = din("identf", [128, 128])
    MASKS = din("masks", [128, 2, 128])
    EDGE = din("edge", [128, 4])
    W_MOD = din("w_mod", [DEPTH, D, 6 * D])
    B_MOD = din("b_mod", [DEPTH, 6 * D])
    GCOL = din("gcol", [DEPTH, 128, 24])
    GROW = din("grow", [DEPTH, 12 * 96])
    LAMV = din("lamv", [DEPTH, 4 * 64])
    SINK = din("sink", [DEPTH, 8])
    W_IN = din("w_in", [DEPTH, D, IN_COLS])
    W_UQ = din("w_uq", [DEPTH, 256, 768])
    W_UKV = din("w_ukv", [DEPTH, 128, 1024])
    W_UP = din("w_up", [DEPTH, 3, 512, D])
    W_O = din("w_o", [DEPTH, D, D])
    W_MI = din("w_mlp_in", [DEPTH, D, 4 * D])
    W_MO = din("w_mlp_out", [DEPTH, 4 * D, D])
    OUT = Tile(nc.dram_tensor("out", [NOWN * 128, D], F32, kind="ExternalOutput").ap())

    X1 = dscr("x1", [T, D], F32)
    XM = dscr("xm", [T, D], F32)
    MODR = dscr("modr", [DEPTH, 2, 6 * D], F32)
    HT = dscr("ht", [8, 128, T])
    QTm = dscr("qtm", [8, 96, T])
    KTm = dscr("ktm", [8, 96, T])
    Vm = dscr("vm", [T, 512])
    QTs = dscr("qts", [8, 128, T])
    KTs = dscr("kts", [128, T])
    Vs = dscr("vs", [T, 128])
    QTd = dscr("qtd", [8, 128, T])
    KTd = dscr("ktd", [4, 128, T])
    Vd = dscr("vd", [T, 512])
    YT = dscr("yt", [1536, T])

    es = ExitStack()
    with es:
        K = KB(nc, es)
        PE, ACT, DVE, POOL = K.PE, K.ACT, K.DVE, K.POOL

        uid = [0]

        def sb(st, name, shape, dt, n=1):
            uid[0] += 1
            tl = [Tile(st.enter_context(nc.sbuf_tensor("%s_u%d_%d" % (name, uid[0], i), list(shape), dt))) for i in range(n)]
            return tl if n > 1 else tl[0]

        PS = [Tile(es.enter_context(nc.psum_tensor("ps%d" % i, [128, 512], F32))) for i in range(8)]
        ps_i = [0]

        ps_pool = [list(PS)]
        tr_pool = [PS[5:8]]
        tr_i = [0]

        def ps():
            pool = ps_pool[0]
            p = pool[ps_i[0] % len(pool)]
            ps_i[0] += 1
            return p

        def ps_tr():
            pool = tr_pool[0]
            p = pool[tr_i[0] % len(pool)]
            tr_i[0] += 1
            return p

        identf = sb(es, "identf", [128, 128], F32)
        identb = sb(es, "identb", [128, 128], BF16)
        selF = sb(es, "selF", [128, 128], F32)
        ones64 = sb(es, "ones64", [64, 128], F32)
        masks = sb(es, "masks", [128, 6, 128], BF16)
        edge = sb(es, "edge", [128, 4], F32)
        scT = sb(es, "scT", [128, 16], F32)
        K.dma("sp", identf[:], IDF[:], reads=[IDF], writes=[identf])
        K.dma("pool", identb[:], IDF[:], reads=[IDF], writes=[identb])
        K.dma("pool", masks[:, 0:2, :], MASKS[:], reads=[MASKS], writes=[masks])
        K.dma("sp", edge[:], EDGE[:], reads=[EDGE], writes=[edge])
        K.dma("sp", scT[:], CT[:], reads=[CT], writes=[scT])
        K.op(DVE, lambda e: e.memset(selF[:], 0.0), writes=[selF])
        K.op(DVE, lambda e: e.memset(selF[64:65, :], 1.0), partial=[selF])
        K.op(DVE, lambda e: e.memset(ones64[:], 1.0), writes=[ones64])
        for i, src in ((2, 0), (3, 1), (4, 0), (5, 1)):
            K.op(DVE, lambda e, i=i, src=src: e.tensor_scalar(masks[:, i, :], masks[:, src, :], edge[:, i - 2:i - 1], None, ALU.mult),
                 reads=[edge, masks], partial=[masks])
        K.op(ACT, lambda e: e.activation(out=scT[:], in_=scT[:], func=AF.Silu), writes=[scT])

        def swa_mask(j, side):
            if side == 0:
                if j == 0:
                    return 2
                if j == NLT // 2:
                    return 4
                return 0
            if j == NLT - 1:
                return 3
            if j == NLT // 2 - 1:
                return 5
            return 1

        for l in range(NLAYERS):
            last = l == DEPTH - 1
            XSRC = XIN if l == 0 else X1
            XDST = OUT if last else X1
            if last:
                q_tiles = list(range(NCTX_T, NCTX_T + NOWN))
            else:
                q_tiles = list(range(TT))
            q_set = set(q_tiles)
            kv_tiles = list(range(TT))
            chunks = []
            if 0 in q_set:
                chunks.append([0, 1])
            lat_q = [t for t in q_tiles if t >= NCTX_T]
            for i in range(0, len(lat_q), 4):
                chunks.append(lat_q[i:i + 4])

            def dst_rows(t):
                return (t - NCTX_T) if last else t

            with ExitStack() as st:
                wm = sb(st, "wm", [128, 8, 512], F32, 2)
                modrow = sb(st, "modrow", [2, 6 * D], F32)
                bmod2 = sb(st, "bmod2", [2, 6 * D], F32)
                K.dma("sp", bmod2[:], B_MOD.t[l:l + 1, :].partition_broadcast(2),
                      reads=[B_MOD], writes=[bmod2])
                for blk in range(12):
                    w = wm[blk % 2]
                    K.dma("sp", w[:], W_MOD.t[l, :, blk * 512:(blk + 1) * 512].rearrange("(k p) c -> p k c", p=128),
                          reads=[W_MOD], writes=[w])
                    p = ps()
                    for kc in range(8):
                        K.op(PE, lambda e, kc=kc, p=p, w=w: e.matmul(p[0:2, :], scT[:, 2 * kc:2 * kc + 2], w[:, kc, :],
                                                                  start=(kc == 0), stop=(kc == 7)),
                             reads=[scT, w], **({"writes": [p]} if kc == 0 else {"partial": [p]}))
                    K.op(DVE, lambda e, p=p, blk=blk: e.tensor_tensor(out=modrow[:, blk * 512:(blk + 1) * 512], in0=p[0:2, :],
                                                                     in1=bmod2[:, blk * 512:(blk + 1) * 512], op=ALU.add),
                         reads=[p, bmod2], **({"writes": [modrow]} if blk == 0 else {"partial": [modrow]}))
                K.dma("pool", MODR.t[l], modrow[:], reads=[modrow], writes=[MODR])

            K.barrier()

            def load_cols(dst, vi):
                for ver in range(2):
                    K.dma("sp", dst[:, :, ver], MODR.t[l, ver, vi * D:(vi + 1) * D].rearrange("(k p) -> p k", p=128),
                          reads=[MODR], **({"writes": [dst]} if ver == 0 else {"partial": [dst]}), slow=True)

            def mod_cols(st, vi_shift, vi_scale, gcol0):
                A = sb(st, "colA", [128, 8, 2], F32)
                Bc = sb(st, "colB", [128, 8, 2], F32)
                gc = sb(st, "gcolt", [128, 24], F32)
                K.dma("sp", gc[:], GCOL.t[l], reads=[GCOL], writes=[gc])
                load_cols(A, vi_scale)
                load_cols(Bc, vi_shift)
                K.op(DVE, lambda e: e.tensor_scalar(A[:], A[:], 1.0, None, ALU.add), writes=[A])
                K.op(DVE, lambda e: e.tensor_tensor(out=A[:], in0=A[:], in1=gc[:, gcol0:gcol0 + 8].unsqueeze(2).to_broadcast([128, 8, 2]),
                                                    op=ALU.mult), reads=[gc], writes=[A])
                return A, Bc, gc

            def norm_transpose(x_ap, x_tile, junk, ssq, xn, A, Bc, ver, dst_fn, dst_tile, first_write):
                K.op(ACT, lambda e: e.activation(out=junk[:], in_=x_ap, func=AF.Square, accum_out=ssq[:, 0:1]),
                     reads=[x_tile], writes=[junk, ssq])
                K.op(DVE, lambda e: e.tensor_scalar(ssq[:, 1:2], ssq[:, 0:1], 1.0 / D, EPS, ALU.mult, ALU.add), writes=[ssq])
                K.op(ACT, lambda e: e.activation(out=ssq[:, 2:3], in_=ssq[:, 1:2], func=AF.Sqrt), writes=[ssq])
                K.op(DVE, lambda e: e.reciprocal(ssq[:, 3:4], ssq[:, 2:3]), writes=[ssq])
                K.op(DVE, lambda e: e.tensor_scalar(xn[:], x_ap, ssq[:, 3:4], None, ALU.mult), reads=[x_tile, ssq], writes=[xn])
                for half in range(2):
                    p = ps()
                    for j in range(4):
                        kc = half * 4 + j
                        K.op(PE, lambda e, p=p, j=j, kc=kc: e.transpose(p[:, j * 128:(j + 1) * 128], xn[:, kc * 128:(kc + 1) * 128], identf[:]),
                             reads=[xn, identf], **({"writes": [p]} if j == 0 else {"partial": [p]}))
                    for j in range(4):
                        kc = half * 4 + j
                        fw = first_write and kc == 0
                        K.op(ACT, lambda e, p=p, j=j, kc=kc: e.activation(out=dst_fn(kc), in_=p[:, j * 128:(j + 1) * 128], func=AF.Identity,
                                                                         scale=A[:, kc, ver:ver + 1], bias=Bc[:, kc, ver:ver + 1]),
                             reads=[p, A, Bc], **({"writes": [dst_tile]} if fw else {"partial": [dst_tile]}))

            with ExitStack() as st:
                ps_pool[0] = PS[0:5]
                A1, B1, gc = mod_cols(st, 0, 1, 0)
                Win = sb(st, "Win", [128, 8, N_TM], BF16)
                for kc in range(8):
                    K.dma("pool", Win[:, kc, :], W_IN.t[l, kc * 128:(kc + 1) * 128, 0:N_TM], reads=[W_IN],
                          **({"writes": [Win]} if kc == 0 else {"partial": [Win]}))
                wuq_f = sb(st, "wuq_f", [128, 2, 768], F32)
                wukv_f = sb(st, "wukv_f", [128, 1024], F32)
                wuq = sb(st, "wuq", [128, 2, 768], BF16)
                wukv = sb(st, "wukv", [128, 1024], BF16)
                K.dma("sp", wuq_f[:], W_UQ.t[l].rearrange("(c p) n -> p c n", p=128), reads=[W_UQ], writes=[wuq_f])
                K.dma("sp", wukv_f[:], W_UKV.t[l], reads=[W_UKV], writes=[wukv_f])
                for c in range(2):
                    K.op(DVE, lambda e, c=c: e.tensor_scalar(wuq[:, c, :], wuq_f[:, c, :], gc[:, 16 + c:17 + c], None, ALU.mult),
                         reads=[wuq_f, gc], **({"writes": [wuq]} if c == 0 else {"partial": [wuq]}))
                K.op(DVE, lambda e: e.tensor_scalar(wukv[:], wukv_f[:], gc[:, 18:19], None, ALU.mult), reads=[wukv_f, gc], writes=[wukv])
                gbc = sb(st, "gbc", [128, 12, 96], F32)
                K.dma("sp", gbc[:].rearrange("p g d -> p (g d)"), GROW.t[l:l + 1, :].partition_broadcast(128),
                      reads=[GROW], writes=[gbc])

                xt = sb(st, "xt", [128, D], F32, 2)
                rp = sb(st, "rp", [128, 192], F32, 2)
                junk = sb(st, "junk", [128, D], BF16)
                ssq = sb(st, "ssq", [128, 4], F32, 2)
                xn = sb(st, "xn", [128, D], F32)
                hT = sb(st, "hT", [128, 8, 128], BF16, 2)
                latT = sb(st, "latT", [128, 3, 128], BF16)
                st8 = sb(st, "st8", [128, 16], F32, 2)
                kpe = sb(st, "kpe", [128, 32], F32)
                tabs = sb(st, "tabs", [128, 6, 2, 96], F32, 2)
                xsrc = sb(st, "xsrc", [128, 768], F32, 4)
                grp = sb(st, "grp", [128, 1664], F32, 2)
                sqv = sb(st, "sqv", [128, 768], F32, 3)
                hs = sb(st, "hs", [128, 8, 4], F32, 12)
                xnn = sb(st, "xnn", [128, 768], F32, 3)
                t1 = sb(st, "t1", [128, 768], F32, 3)
                t2 = sb(st, "t2", [128, 512], F32, 3)
                MQ = sb(st, "MQ", [128, 8, 96], BF16, 2)
                MK = sb(st, "MK", [128, 8, 96], BF16, 2)
                SQp = sb(st, "SQp", [128, 8, 128], BF16, 2)
                DQp = sb(st, "DQp", [128, 8, 128], BF16, 2)
                SKt = sb(st, "SKt", [128, 128], BF16, 2)
                DKt = sb(st, "DKt", [128, 512], BF16, 2)
                Vm_s = sb(st, "Vm_s", [128, 8, 64], BF16, 2)
                Vs_s = sb(st, "Vs_s", [128, 128], BF16, 2)
                Vd_s = sb(st, "Vd_s", [128, 512], BF16, 2)
                o_qtm = sb(st, "o_qtm", [128, 8, 128], BF16, 2)
                o_ktm = sb(st, "o_ktm", [128, 8, 128], BF16, 2)
                o_qts = sb(st, "o_qts", [128, 8, 128], BF16, 2)
                o_qtd = sb(st, "o_qtd", [128, 8, 128], BF16, 2)
                o_kt = sb(st, "o_kt", [128, 5, 128], BF16, 2)
                for i in range(2):
                    K.op(POOL, lambda e, i=i: e.memset(SQp[i][:], 0.0), writes=[SQp[i]])
                    K.op(POOL, lambda e, i=i: e.memset(DQp[i][:], 0.0), writes=[DQp[i]])

                def hnr(it, k, src_ap, src_tiles, H, d, gi, pieces, dst_tile, rope_lo):
                    n = H * d
                    hst = hs[(it % 2) * 6 + gi]
                    tb = tabs[it % 2]
                    sqv_, xnn_, t1_, t2_ = sqv[k], xnn[k], t1[k], t2[k]
                    K.op(ACT, lambda e: e.activation(out=sqv_[:, 0:n], in_=src_ap, func=AF.Square), reads=src_tiles, writes=[sqv_])
                    yield
                    K.op(DVE, lambda e: e.tensor_reduce(out=hst[:, 0:H, 0], in_=sqv_[:, 0:n].rearrange("p (h d) -> p h d", h=H), axis=AX.X, op=ALU.add),
                         reads=[sqv_], writes=[hst])
                    yield
                    K.op(DVE, lambda e: e.tensor_scalar(hst[:, 0:H, 1], hst[:, 0:H, 0], 1.0 / d, EPS, ALU.mult, ALU.add), writes=[hst])
                    yield
                    K.op(ACT, lambda e: e.activation(out=hst[:, 0:H, 2], in_=hst[:, 0:H, 1], func=AF.Sqrt), writes=[hst])
                    yield
                    K.op(DVE, lambda e: e.reciprocal(hst[:, 0:H, 3], hst[:, 0:H, 2]), writes=[hst])
                    yield
                    K.op(DVE, lambda e: e.tensor_tensor(out=xnn_[:, 0:n].rearrange("p (h d) -> p h d", h=H),
                                                        in0=src_ap.rearrange("p (h d) -> p h d", h=H),
                                                        in1=hst[:, 0:H, 3:4].to_broadcast([128, H, d]), op=ALU.mult),
                         reads=src_tiles + [hst], writes=[xnn_])
                    yield
                    xv = xnn_[:, 0:n].rearrange("p (h d) -> p h d", h=H)
                    cg = tb[:, gi, 0, 0:d].unsqueeze(1).to_broadcast([128, H, d])
                    K.op(DVE, lambda e: e.tensor_tensor(out=t1_[:, 0:n].rearrange("p (h d) -> p h d", h=H), in0=xv, in1=cg, op=ALU.mult),
                         reads=[xnn_, tb], writes=[t1_])
                    yield
                    nr = d - rope_lo
                    q4 = nr // 4
                    t2v = t2_[:, 0:H * nr].rearrange("p (h d) -> p h d", h=H)
                    for blk in range(2):
                        for hf in range(2):
                            lo_dst = rope_lo + blk * 2 * q4 + hf * q4
                            lo_src = rope_lo + blk * 2 * q4 + (1 - hf) * q4
                            first = blk == 0 and hf == 0
                            K.op(POOL if hf else DVE, lambda e, lo_dst=lo_dst, lo_src=lo_src: e.tensor_tensor(
                                out=t2v[:, :, lo_dst - rope_lo:lo_dst - rope_lo + q4],
                                in0=xv[:, :, lo_src:lo_src + q4],
                                in1=tb[:, gi, 1, lo_dst:lo_dst + q4].unsqueeze(1).to_broadcast([128, H, q4]), op=ALU.mult),
                                reads=[xnn_, tb], **({"writes": [t2_]} if first else {"partial": [t2_]}))
                            yield
                    t1v = t1_[:, 0:n].rearrange("p (h d) -> p h d", h=H)
                    first = True
                    for (hsel, dst_fn) in pieces:
                        if rope_lo > 0:
                            K.op(ACT, lambda e, hsel=hsel, dst_fn=dst_fn: e.activation(out=dst_fn(0, rope_lo), in_=t1v[:, hsel, 0:rope_lo], func=AF.Copy),
                                 reads=[t1_], **({"writes": [dst_tile]} if first else {"partial": [dst_tile]}))
                            first = False
                            yield
                        K.op(DVE, lambda e, hsel=hsel, dst_fn=dst_fn: e.tensor_tensor(out=dst_fn(rope_lo, d), in0=t1v[:, hsel, rope_lo:d],
                                                                                       in1=t2v[:, hsel, :], op=ALU.add),
                             reads=[t1_, t2_], **({"writes": [dst_tile]} if first else {"partial": [dst_tile]}))
                        first = False
                        yield

                def interleave(gens):
                    gens = list(gens)
                    while gens:
                        for g in list(gens):
                            try:
                                next(g)
                            except StopIteration:
                                gens.remove(g)

                def front(it, t):
                    need_q = t in q_set
                    ver = 1 if t < NCTX_T else 0
                    s = it % 2
                    x_t, rp_t, hT_t = xt[s], rp[s], hT[s]
                    K.dma("sp", x_t[:], XSRC.t[t * 128:(t + 1) * 128, :], reads=[XSRC], writes=[x_t])
                    K.dma("sp", rp_t[:], ROPE.t[t * 128:(t + 1) * 128, :], reads=[ROPE], writes=[rp_t])
                    tb = tabs[s]
                    first = True
                    for gi, (d, c_lo, s_lo) in enumerate(((96, 128, 160), (96, 128, 160), (64, 0, 64), (64, 0, 64), (64, 0, 64), (64, 0, 64))):
                        kw = {"writes": [tb]} if first else {"partial": [tb]}
                        first = False
                        if d == 96:
                            K.op(POOL, lambda e, gi=gi: e.tensor_copy(tb[:, gi, 0, 0:64], gbc[:, 2 * gi, 0:64]), reads=[gbc], **kw)
                            K.op(POOL, lambda e, gi=gi, c_lo=c_lo: e.tensor_tensor(out=tb[:, gi, 0, 64:96], in0=rp_t[:, c_lo:c_lo + 32],
                                                                                 in1=gbc[:, 2 * gi, 64:96], op=ALU.mult),
                                 reads=[gbc, rp_t], partial=[tb])
                            K.op(POOL, lambda e, gi=gi, s_lo=s_lo: e.tensor_tensor(out=tb[:, gi, 1, 64:96], in0=rp_t[:, s_lo:s_lo + 32],
                                                                                 in1=gbc[:, 2 * gi + 1, 64:96], op=ALU.mult),
                                 reads=[gbc, rp_t], partial=[tb])
                        else:
                            K.op(POOL, lambda e, gi=gi, c_lo=c_lo: e.tensor_tensor(out=tb[:, gi, 0, 0:64], in0=rp_t[:, c_lo:c_lo + 64],
                                                                                 in1=gbc[:, 2 * gi, 0:64], op=ALU.mult),
                                 reads=[gbc, rp_t], **kw)
                            K.op(POOL, lambda e, gi=gi, s_lo=s_lo: e.tensor_tensor(out=tb[:, gi, 1, 0:64], in0=rp_t[:, s_lo:s_lo + 64],
                                                                                 in1=gbc[:, 2 * gi + 1, 0:64], op=ALU.mult),
                                 reads=[gbc, rp_t], partial=[tb])
                    norm_transpose(x_t[:], x_t, junk, ssq[s], xn, A1, B1, ver, lambda kc: hT_t[:, kc, :], hT_t, True)
                    if need_q:
                        K.dma("pool", HT.t[:, :, t * 128:(t + 1) * 128].rearrange("k p t -> p k t"), hT_t[:], reads=[hT_t], partial=[HT])

                    def proj_tm(c0, c1):
                        p = ps()
                        for kc in range(8):
                            K.op(PE, lambda e, kc=kc, p=p: e.matmul(p[:, 0:c1 - c0], hT_t[:, kc, :], Win[:, kc, c0:c1], start=(kc == 0), stop=(kc == 7)),
                                 reads=[hT_t, Win], **({"writes": [p]} if kc == 0 else {"partial": [p]}))
                        return p

                    st_t = st8[s]
                    c_lo = 0 if need_q else 256
                    p0 = proj_tm(c_lo, 416)
                    off = -c_lo
                    K.op(DVE, lambda e: e.memset(st_t[:, 0:2], 1.0), writes=[st_t])
                    if need_q:
                        K.op(ACT, lambda e: e.activation(out=junk[:, 0:256], in_=p0[:, 0:256], func=AF.Square, accum_out=st_t[:, 0:1]),
                             reads=[p0], writes=[junk, st_t])
                    K.op(ACT, lambda e: e.activation(out=junk[:, 256:384], in_=p0[:, 256 + off:384 + off], func=AF.Square, accum_out=st_t[:, 1:2]),
                         reads=[p0], writes=[junk, st_t])
                    K.op(ACT, lambda e: e.activation(out=kpe[:], in_=p0[:, 384 + off:416 + off], func=AF.Copy), reads=[p0], writes=[kpe])
                    K.op(DVE, lambda e: e.tensor_scalar(st_t[:, 2:3], st_t[:, 0:1], 1.0 / 256, EPS, ALU.mult, ALU.add), writes=[st_t])
                    K.op(DVE, lambda e: e.tensor_scalar(st_t[:, 3:4], st_t[:, 1:2], 1.0 / 128, EPS, ALU.mult, ALU.add), writes=[st_t])
                    K.op(ACT, lambda e: e.activation(out=st_t[:, 4:6], in_=st_t[:, 2:4], func=AF.Sqrt), writes=[st_t])
                    K.op(DVE, lambda e: e.reciprocal(st_t[:, 6:8], st_t[:, 4:6]), writes=[st_t])
                    pl = ps()
                    lat_chunks = [0, 1, 2] if need_q else [2]
                    for ci, c in enumerate(lat_chunks):
                        for kc in range(8):
                            K.op(PE, lambda e, kc=kc, c=c: e.matmul(pl[:, c * 128:(c + 1) * 128], Win[:, kc, c * 128:(c + 1) * 128], hT_t[:, kc, :],
                                                                   start=(kc == 0), stop=(kc == 7)),
                                 reads=[hT_t, Win], **({"writes": [pl]} if (kc == 0 and ci == 0) else {"partial": [pl]}))
                    lo = lat_chunks[0] * 128
                    K.op(ACT, lambda e: e.activation(out=latT[:].rearrange("p c t -> p (c t)")[:, lo:384], in_=pl[:, lo:384], func=AF.Copy),
                         reads=[pl], writes=[latT])
                    g_t = grp[s]
                    first = True
                    if need_q:
                        p = proj_tm(416, 928)
                        K.op(ACT, lambda e, p=p: e.activation(out=g_t[:, 0:512], in_=p[:, 0:512], func=AF.Copy), reads=[p], writes=[g_t])
                        first = False
                    p = proj_tm(928, 1184)
                    K.op(ACT, lambda e, p=p: e.activation(out=g_t[:, 512:640], in_=p[:, 0:128], func=AF.Copy), reads=[p],
                         **({"writes": [g_t]} if first else {"partial": [g_t]}))
                    vs_t = Vs_s[s]
                    K.op(ACT, lambda e, p=p: e.activation(out=vs_t[:], in_=p[:, 128:256], func=AF.Copy), reads=[p], writes=[vs_t])
                    K.dma("pool", Vs.t[t * 128:(t + 1) * 128, :], vs_t[:], reads=[vs_t], partial=[Vs])
                    if need_q:
                        p = proj_tm(1184, 1696)
                        K.op(ACT, lambda e, p=p: e.activation(out=g_t[:, 640:1152], in_=p[:, 0:512], func=AF.Copy), reads=[p], partial=[g_t])
                    p = proj_tm(1696, 2208)
                    K.op(ACT, lambda e, p=p: e.activation(out=g_t[:, 1152:1664], in_=p[:, 0:512], func=AF.Copy), reads=[p], partial=[g_t])
                    p = proj_tm(2208, 2720)
                    vd_t = Vd_s[s]
                    K.op(ACT, lambda e, p=p: e.activation(out=vd_t[:], in_=p[:, :], func=AF.Copy), reads=[p], writes=[vd_t])
                    K.dma("pool", Vd.t[t * 128:(t + 1) * 128, :], vd_t[:], reads=[vd_t], partial=[Vd])
                    if need_q:
                        xq = xsrc[2 * s]
                        for (n0, n1) in ((0, 512), (512, 768)):
                            p = ps()
                            for c in range(2):
                                K.op(PE, lambda e, c=c, p=p, n0=n0, n1=n1: e.matmul(p[:, 0:n1 - n0], latT[:, c, :], wuq[:, c, n0:n1], start=(c == 0), stop=(c == 1)),
                                     reads=[latT, wuq], **({"writes": [p]} if c == 0 else {"partial": [p]}))
                            K.op(ACT, lambda e, p=p, n0=n0, n1=n1: e.activation(out=xq[:, n0:n1], in_=p[:, 0:n1 - n0], func=AF.Identity, scale=st_t[:, 6:7]),
                                 reads=[p, st_t], **({"writes": [xq]} if n0 == 0 else {"partial": [xq]}))
                    xk = xsrc[2 * s + 1]
                    xkv = xk[:, 0:768].rearrange("p (h d) -> p h d", h=8)
                    vm_t = Vm_s[s]
                    for hh in range(2):
                        p = ps()
                        K.op(PE, lambda e, p=p, hh=hh: e.matmul(p[:, :], latT[:, 2, :], wukv[:, hh * 512:(hh + 1) * 512], start=True, stop=True),
                             reads=[latT, wukv], writes=[p])
                        pv = p[:, :].rearrange("p (h d) -> p h d", h=4)
                        K.op(ACT, lambda e, pv=pv, hh=hh: e.activation(out=xkv[:, hh * 4:(hh + 1) * 4, 0:64], in_=pv[:, :, 0:64], func=AF.Identity, scale=st_t[:, 7:8]),
                             reads=[p, st_t], **({"writes": [xk]} if hh == 0 else {"partial": [xk]}))
                        K.op(ACT, lambda e, pv=pv, hh=hh: e.activation(out=vm_t[:, hh * 4:(hh + 1) * 4, :], in_=pv[:, :, 64:128], func=AF.Identity, scale=st_t[:, 7:8]),
                             reads=[p, st_t], **({"writes": [vm_t]} if hh == 0 else {"partial": [vm_t]}))
                    K.op(POOL, lambda e: e.tensor_copy(xkv[:, :, 64:96], kpe[:].unsqueeze(1).to_broadcast([128, 8, 32])), reads=[kpe], partial=[xk])
                    K.dma("pool", Vm.t[t * 128:(t + 1) * 128, :], vm_t[:].rearrange("p h d -> p (h d)"), reads=[vm_t], partial=[Vm])

                def back_h(it, t):
                    need_q = t in q_set
                    s = it % 2
                    g_t = grp[s]
                    xq, xk = xsrc[2 * s], xsrc[2 * s + 1]
                    mq_t, mk_t, sq_t, sk_t, dq_t, dk_t = MQ[s], MK[s], SQp[s], SKt[s], DQp[s], DKt[s]
                    dqv = dq_t[:].rearrange("p (h j) c -> p h j c", j=2)
                    g_mk = lambda k: hnr(it, k, xk[:, 0:768], [xk], 8, 96, 1, [(slice(0, 8), lambda a, b_: mk_t[:, :, a:b_])], mk_t, 64)
                    g_sk = lambda k: hnr(it, k, g_t[:, 512:640], [g_t], 2, 64, 3,
                                         [(slice(0, 2), lambda a, b_: sk_t[:].rearrange("p (h d) -> p h d", h=2)[:, :, a:b_])], sk_t, 0)
                    g_dk = lambda k: hnr(it, k, g_t[:, 1152:1664], [g_t], 8, 64, 5,
                                         [(slice(0, 8), lambda a, b_: dk_t[:].rearrange("p (h d) -> p h d", h=8)[:, :, a:b_])], dk_t, 0)
                    if need_q:
                        g_mq = lambda k: hnr(it, k, xq[:, 0:768], [xq], 8, 96, 0, [(slice(0, 8), lambda a, b_: mq_t[:, :, a:b_])], mq_t, 64)
                        g_sq = lambda k: hnr(it, k, g_t[:, 0:512], [g_t], 8, 64, 2,
                                             [(slice(0, 4), lambda a, b_: sq_t[:, 0:4, a:b_]),
                                              (slice(4, 8), lambda a, b_: sq_t[:, 4:8, 64 + a:64 + b_])], sq_t, 0)
                        g_dq = lambda k: hnr(it, k, g_t[:, 640:1152], [g_t], 8, 64, 4,
                                             [(slice(0, 8, 2), lambda a, b_: dqv[:, :, 0, a:b_]),
                                              (slice(1, 8, 2), lambda a, b_: dqv[:, :, 1, 64 + a:64 + b_])], dq_t, 0)
                        interleave([g_mq(0), g_mk(1), g_sq(2)])
                        interleave([g_dq(0), g_dk(1), g_sk(2)])
                    else:
                        interleave([g_mk(0), g_dk(1), g_sk(2)])

                def back_tr(it, t):
                    need_q = t in q_set
                    s = it % 2

                    def tr_group(src_fn, nblk, rows, dst_tile, dst_slot0, srcs, use_dve):
                        p = ps_tr()
                        pb = p[:, :].bitcast(BF16)
                        for j in range(nblk):
                            K.op(PE, lambda e, j=j: e.transpose(pb[0:rows, j * 128:(j + 1) * 128], src_fn(j), identb[:]),
                                 reads=srcs + [identb], **({"writes": [p]} if j == 0 else {"partial": [p]}))
                        dst = dst_tile[0:rows, dst_slot0:dst_slot0 + nblk, :].rearrange("p a t -> p (a t)")
                        kw = {"writes": [dst_tile]} if dst_slot0 == 0 else {"partial": [dst_tile]}
                        if use_dve:
                            K.op(DVE, lambda e: e.tensor_copy(dst, pb[0:rows, 0:nblk * 128]), reads=[p], **kw)
                        else:
                            K.op(ACT, lambda e: e.activation(out=dst, in_=pb[0:rows, 0:nblk * 128], func=AF.Copy), reads=[p], **kw)

                    tsl = slice(t * 128, (t + 1) * 128)
                    if need_q:
                        tr_group(lambda j: MQ[s][:, j, :], 8, 96, o_qtm[s], 0, [MQ[s]], False)
                        K.dma("pool", QTm.t[:, :, tsl].rearrange("h d t -> d h t"), o_qtm[s][0:96, :, :], reads=[o_qtm[s]], partial=[QTm])
                        tr_group(lambda j: SQp[s][:, j, :], 8, 128, o_qts[s], 0, [SQp[s]], False)
                        K.dma("pool", QTs.t[:, :, tsl].rearrange("h d t -> d h t"), o_qts[s][:], reads=[o_qts[s]], partial=[QTs])
                        tr_group(lambda j: DQp[s][:, j, :], 8, 128, o_qtd[s], 0, [DQp[s]], False)
                        K.dma("pool", QTd.t[:, :, tsl].rearrange("h d t -> d h t"), o_qtd[s][:], reads=[o_qtd[s]], partial=[QTd])
                    tr_group(lambda j: MK[s][:, j, :], 8, 96, o_ktm[s], 0, [MK[s]], False)
                    K.dma("pool", KTm.t[:, :, tsl].rearrange("h d t -> d h t"), o_ktm[s][0:96, :, :], reads=[o_ktm[s]], partial=[KTm])
                    tr_group(lambda j: DKt[s][:, j * 128:(j + 1) * 128], 4, 128, o_kt[s], 0, [DKt[s]], False)
                    tr_group(lambda j: SKt[s][:, :], 1, 128, o_kt[s], 4, [SKt[s]], False)
                    K.dma("pool", KTd.t[:, :, tsl].rearrange("h d t -> d h t"), o_kt[s][:, 0:4, :], reads=[o_kt[s]], partial=[KTd])
                    K.dma("pool", KTs.t[:, tsl], o_kt[s][:, 4, :], reads=[o_kt[s]], partial=[KTs])

                nkv = len(kv_tiles)
                for it in range(nkv + 2):
                    if it < nkv:
                        front(it, kv_tiles[it])
                    if 1 <= it <= nkv:
                        back_h(it - 1, kv_tiles[it - 1])
                    if it >= 2:
                        back_tr(it - 2, kv_tiles[it - 2])
                ps_pool[0] = list(PS)

            K.barrier()
            lam_init = 0.8 - 0.6 * math.exp(-0.3 * l)
            with ExitStack() as st:
                PT = sb(st, "PT", [128, 512], BF16, 6)
                osb = sb(st, "osb", [128, 512], F32, 6)
                yst = sb(st, "yst", [128, 512], BF16, 4)
                cnt = {"s": 0, "pt": 0, "o": 0, "y": 0, "m": 0, "j": 0}
                MISC = [PS[6], PS[7]]
                pending = []

                def misc():
                    p = MISC[cnt["m"] % 2]
                    cnt["m"] += 1
                    return p

                def defer(fin):
                    while pending:
                        pending.pop(0)()
                    if fin is not None:
                        pending.append(fin)

                lamt = sb(st, "lamt", [128, 4, 64], F32)
                lsc = sb(st, "lsc", [128, 8], F32)
                K.dma("sp", lamt[:].rearrange("p a d -> p (a d)"), LAMV.t[l:l + 1, :].partition_broadcast(128), reads=[LAMV], writes=[lamt])
                K.op(DVE, lambda e: e.tensor_tensor(out=lamt[:, 0, :], in0=lamt[:, 0, :], in1=lamt[:, 1, :], op=ALU.mult), writes=[lamt])
                K.op(DVE, lambda e: e.tensor_tensor(out=lamt[:, 2, :], in0=lamt[:, 2, :], in1=lamt[:, 3, :], op=ALU.mult), writes=[lamt])
                K.op(DVE, lambda e: e.tensor_reduce(out=lsc[:, 0:1], in_=lamt[:, 0, :], axis=AX.X, op=ALU.add), reads=[lamt], writes=[lsc])
                K.op(DVE, lambda e: e.tensor_reduce(out=lsc[:, 1:2], in_=lamt[:, 2, :], axis=AX.X, op=ALU.add), reads=[lamt], writes=[lsc])
                K.op(ACT, lambda e: e.activation(out=lsc[:, 2:4], in_=lsc[:, 0:2], func=AF.Exp), writes=[lsc])
                K.op(DVE, lambda e: e.tensor_tensor(out=lsc[:, 4:5], in0=lsc[:, 3:4], in1=lsc[:, 2:3], op=ALU.subtract), writes=[lsc])
                K.op(DVE, lambda e: e.tensor_scalar(lsc[:, 5:6], lsc[:, 4:5], -lam_init, None, ALU.add), writes=[lsc])
                neglam = lsc[:, 5:6]
                gcs = sb(st, "gcs", [128, 24], F32)
                K.dma("sp", gcs[:], GCOL.t[l], reads=[GCOL], writes=[gcs])
                K.op(DVE, lambda e: e.tensor_scalar(gcs[:, 21:22], gcs[:, 21:22], 1.0 - lam_init, None, ALU.mult), writes=[gcs])
                sk8 = sb(st, "sk8", [128, 8], F32)
                sinkrow = sb(st, "sinkrow", [128, 8, 128], F32)
                K.dma("sp", sk8[:], SINK.t[l:l + 1, :].partition_broadcast(128), reads=[SINK], writes=[sk8])
                K.op(ACT, lambda e: e.activation(out=sk8[:], in_=sk8[:], func=AF.Exp), writes=[sk8])
                K.op(DVE, lambda e: e.tensor_copy(sinkrow[:], sk8[:].unsqueeze(2).to_broadcast([128, 8, 128])), reads=[sk8], writes=[sinkrow])

                def run_job(maps, kts, n, scale, mask_fn=None):
                    nk = len(kts)
                    lag = 2 if len(maps) == 1 else 1
                    hist = []
                    for i in range(nk + lag):
                        cur = None
                        if i < nk:
                            kt = kts[i]
                            cur = []
                            for m in maps:
                                sbank = PS[cnt["s"] % 3]
                                cnt["s"] += 1
                                K.op(PE, lambda e, m=m, kt=kt, sbank=sbank: e.matmul(sbank[:, 0:n], m["kt_fn"](kt), m["q_ap"], start=True, stop=True),
                                     reads=m["kt_tiles"] + m["q_tiles"], writes=[sbank])
                                pt = PT[cnt["pt"] % 6]
                                cnt["pt"] += 1
                                K.op(ACT, lambda e, pt=pt, sbank=sbank: e.activation(out=pt[:, 0:n], in_=sbank[:, 0:n], func=AF.Exp, scale=scale),
                                     reads=[sbank], writes=[pt])
                                mi = mask_fn(kt) if mask_fn else None
                                if mi is not None:
                                    K.op(DVE, lambda e, pt=pt, mi=mi: e.tensor_tensor(
                                        out=pt[:, 0:n].rearrange("p (h t) -> p h t", t=128), in0=pt[:, 0:n].rearrange("p (h t) -> p h t", t=128),
                                        in1=masks[:, mi, :].unsqueeze(1).to_broadcast([128, n // 128, 128]), op=ALU.mult),
                                        reads=[masks], writes=[pt])
                                for (AE, acc_t, c0, c1) in m.get("acc", ()):
                                    if i == 0:
                                        K.op(AE, lambda e, pt=pt, acc_t=acc_t, c0=c0, c1=c1: e.tensor_copy(acc_t[:, 0:c1 - c0], pt[:, c0:c1]), reads=[pt], writes=[acc_t])
                                    else:
                                        K.op(AE, lambda e, pt=pt, acc_t=acc_t, c0=c0, c1=c1: e.tensor_tensor(out=acc_t[:, 0:c1 - c0], in0=acc_t[:, 0:c1 - c0], in1=pt[:, c0:c1], op=ALU.add),
                                             reads=[pt], writes=[acc_t])
                                cur.append(pt)
                            hist.append(cur)
                        if i >= lag:
                            ip = i - lag
                            ktp = kts[ip]
                            for m, pt in zip(maps, hist[ip]):
                                for (v_fn, v_tiles, ob, rows) in m["pvs"]:
                                    K.op(PE, lambda e, v_fn=v_fn, ob=ob, pt=pt, ktp=ktp, rows=rows: e.matmul(ob[0:rows, 0:n], v_fn(ktp), pt[:, 0:n],
                                                                                                            start=(ip == 0), stop=(ip == nk - 1)),
                                         reads=[pt] + v_tiles, **({"writes": [ob]} if ip == 0 else {"partial": [ob]}))

                def evac(ob, n, rows=65):
                    o_s = osb[cnt["o"] % 6]
                    cnt["o"] += 1
                    K.op(ACT, lambda e: e.activation(out=o_s[0:rows, 0:n], in_=ob[0:rows, 0:n], func=AF.Copy), reads=[ob], writes=[o_s])
                    return o_s

                def finish_simple(ob, n, sink_g, store_fn):
                    o_s = evac(ob, n)

                    def fin():
                        if sink_g is not None:
                            K.op(DVE, lambda e: e.tensor_tensor(out=o_s[64:65, 0:n], in0=o_s[64:65, 0:n],
                                                                in1=sinkrow[64:65, sink_g * 4:(sink_g + 1) * 4, :].rearrange("p h t -> p (h t)"), op=ALU.add),
                                 reads=[sinkrow], writes=[o_s])
                        K.op(DVE, lambda e: e.reciprocal(o_s[64:65, 0:n], o_s[64:65, 0:n]), writes=[o_s])
                        bc = misc()
                        K.op(PE, lambda e: e.matmul(bc[:, 0:n], selF[0:65, :], o_s[0:65, 0:n], start=True, stop=True), reads=[selF, o_s], writes=[bc])
                        y = yst[cnt["y"] % 4]
                        cnt["y"] += 1
                        K.op(DVE, lambda e: e.tensor_tensor(out=y[0:64, 0:n], in0=o_s[0:64, 0:n], in1=bc[0:64, 0:n], op=ALU.mult), reads=[o_s, bc], writes=[y])
                        store_fn(y)

                    return fin

                with ExitStack() as st2:
                    Vx = sb(st2, "Vx", [128, TT, 8, 65], BF16)
                    KTt = sb(st2, "KTt", [96, T], BF16, 2)
                    Qt = sb(st2, "Qt", [96, 512], BF16, 3)
                    K.op(POOL, lambda e: e.memset(Vx[:], 1.0), writes=[Vx])
                    for k0 in range(TT):
                        K.dma("sp", Vx[:, k0, :, 0:64], Vm.t[k0 * 128:(k0 + 1) * 128, :].rearrange("p (h d) -> p h d", h=8),
                              reads=[Vm], partial=[Vx])
                    for h in range(8):
                        kt_t = KTt[h % 2]
                        K.dma("sp", kt_t[:], KTm.t[h], reads=[KTm], writes=[kt_t])
                        for ch in chunks:
                            q0, n = ch[0] * 128, len(ch) * 128
                            q_t = Qt[cnt["j"] % 3]
                            ob = PS[3 + cnt["j"] % 2]
                            cnt["j"] += 1
                            K.dma("sp", q_t[:, 0:n], QTm.t[h, :, q0:q0 + n], reads=[QTm], writes=[q_t])
                            kts = [0, 1] if ch[0] < NCTX_T else kv_tiles
                            run_job([dict(kt_fn=lambda kt, kt_t=kt_t: kt_t[:, kt * 128:(kt + 1) * 128], kt_tiles=[kt_t], q_ap=q_t[:, 0:n], q_tiles=[q_t],
                                          pvs=[(lambda kt, h=h: Vx[:, kt, h, :], [Vx], ob, 65)])], kts, n, MLA_SCALE)
                            defer(finish_simple(ob, n, None,
                                                lambda y, h=h, q0=q0, n=n: K.dma("pool", YT.t[h * 64:(h + 1) * 64, q0:q0 + n], y[0:64, 0:n], reads=[y], partial=[YT])))
                    defer(None)

                K.barrier()
                with ExitStack() as st2:
                    Vx = sb(st2, "Vxs", [128, TT, 2, 65], BF16)
                    KTt = sb(st2, "KTst", [128, T], BF16)
                    Qs = sb(st2, "Qs", [128, 8, 128], BF16, 3)
                    K.op(POOL, lambda e: e.memset(Vx[:], 1.0), writes=[Vx])
                    for k0 in range(TT):
                        K.dma("sp", Vx[:, k0, :, 0:64], Vs.t[k0 * 128:(k0 + 1) * 128, :].rearrange("p (h d) -> p h d", h=2),
                              reads=[Vs], partial=[Vx])
                    K.dma("sp", KTt[:], KTs.t[:, :], reads=[KTs], writes=[KTt])
                    for qi, t in enumerate(q_tiles):
                        q_t = Qs[qi % 3]
                        tsl = slice(t * 128, (t + 1) * 128)
                        K.dma("sp", q_t[:], QTs.t[:, :, tsl].rearrange("h p t -> p h t"), reads=[QTs], writes=[q_t])
                        if t < NCTX_T:
                            kts = [0, 1]
                            mk = {}
                        else:
                            j = t - NCTX_T
                            pv_t = NCTX_T + (j - 1) % NLT
                            nx_t = NCTX_T + (j + 1) % NLT
                            kts = [0, 1, pv_t, t, nx_t]
                            mk = {pv_t: swa_mask(j, 0), nx_t: swa_mask(j, 1)}
                        run_job([dict(kt_fn=lambda kt: KTt[:, kt * 128:(kt + 1) * 128], kt_tiles=[KTt],
                                      q_ap=q_t[:, 4 * g:4 * g + 4, :], q_tiles=[q_t],
                                      pvs=[(lambda kt, g=g: Vx[:, kt, g, :], [Vx], PS[3 + g], 65)]) for g in range(2)], kts, 512, HD_SCALE,
                                mask_fn=(lambda kt, mk=mk: mk.get(kt)))
                        fins = [finish_simple(PS[3 + g], 512, g,
                                              lambda y, g=g, tsl=tsl: K.dma("pool", YT.t[512 + g * 256:512 + (g + 1) * 256, tsl].rearrange("(h d) t -> d h t", d=64),
                                                                          y[0:64, 0:512].rearrange("d (h t) -> d h t", h=4), reads=[y], partial=[YT]))
                                for g in range(2)]
                        defer(lambda fins=fins: [f() for f in fins])
                    defer(None)

                K.barrier()
                with ExitStack() as st2:
                    Vx = sb(st2, "Vxd", [128, TT, 128], BF16, 2)
                    KTt = sb(st2, "KTdt", [128, T], BF16, 2)
                    Qd = sb(st2, "Qd", [128, 2, 512], BF16, 3)
                    accD = sb(st2, "accD", [128, 256], F32, 2)
                    accP = sb(st2, "accP", [128, 256], F32, 2)
                    rr = sb(st2, "rrd", [128, 512], F32, 2)
                    sqd = sb(st2, "sqd", [128, 512], F32)
                    lnv = sb(st2, "lnv", [128, 512], F32)
                    onesF = sb(st2, "onesF", [128, 128], F32)
                    onesB = sb(st2, "onesB", [128, 128], BF16)
                    K.op(DVE, lambda e: e.memset(onesF[:], 1.0), writes=[onesF])
                    K.op(DVE, lambda e: e.memset(onesB[:], 1.0), writes=[onesB])
                    jd = 0
                    for h in range(4):
                        vx, kt_t = Vx[h % 2], KTt[h % 2]
                        first = True
                        for k0 in range(0, TT, 16):
                            k1 = min(TT, k0 + 16)
                            K.dma("sp", vx[:, k0:k1, :], Vd.t[k0 * 128:k1 * 128, h * 128:(h + 1) * 128].rearrange("(k p) d -> p k d", p=128),
                                  reads=[Vd], **({"writes": [vx]} if first else {"partial": [vx]}))
                            first = False
                        K.dma("sp", kt_t[:], KTd.t[h], reads=[KTd], writes=[kt_t])
                        for ch in chunks:
                            q0, n = ch[0] * 128, len(ch) * 128
                            q_t = Qd[cnt["j"] % 3]
                            cnt["j"] += 1
                            K.dma("sp", q_t[:, :, 0:n], QTd.t[2 * h:2 * h + 2, :, q0:q0 + n].rearrange("m p t -> p m t"), reads=[QTd], writes=[q_t])
                            kts = [0, 1] if ch[0] < NCTX_T else kv_tiles
                            aD, aP = accD[jd % 2], accP[jd % 2]
                            jd += 1
                            nh = n // 2
                            maps = [
                                dict(kt_fn=lambda kt, kt_t=kt_t: kt_t[:, kt * 128:(kt + 1) * 128], kt_tiles=[kt_t],
                                     q_ap=q_t[:, 0, 0:n], q_tiles=[q_t], acc=[(DVE, aD, 0, nh), (POOL, aP, nh, n)],
                                     pvs=[(lambda kt, vx=vx: vx[:, kt, :], [vx], PS[3], 128)]),
                                dict(kt_fn=lambda kt, kt_t=kt_t: kt_t[:, kt * 128:(kt + 1) * 128], kt_tiles=[kt_t],
                                     q_ap=q_t[:, 1, 0:n], q_tiles=[q_t],
                                     pvs=[(lambda kt, vx=vx: vx[:, kt, :], [vx], PS[4], 128),
                                          (lambda kt: onesB[:, :], [onesB], PS[5], 128)]),
                            ]
                            run_job(maps, kts, n, HD_SCALE)
                            o1, o2, s2 = evac(PS[3], n, 128), evac(PS[4], n, 128), evac(PS[5], n, 128)

                            def fin(o1=o1, o2=o2, s2=s2, aD=aD, aP=aP, n=n, nh=nh, h=h, q0=q0):
                                s1 = misc()
                                K.op(PE, lambda e: e.matmul(s1[:, 0:nh], onesF[:, :], aD[:, 0:nh], start=True, stop=True), reads=[onesF, aD], writes=[s1])
                                K.op(PE, lambda e: e.matmul(s1[:, nh:n], onesF[:, :], aP[:, 0:nh], start=True, stop=True), reads=[onesF, aP], partial=[s1])
                                K.op(DVE, lambda e: e.reciprocal(rr[0][:, 0:n], s1[:, 0:n]), reads=[s1], writes=[rr[0]])
                                K.op(DVE, lambda e: e.reciprocal(rr[1][:, 0:n], s2[:, 0:n]), reads=[s2], writes=[rr[1]])
                                K.op(DVE, lambda e: e.tensor_scalar(rr[1][:, 0:n], rr[1][:, 0:n], neglam, None, ALU.mult), reads=[lsc], writes=[rr[1]])
                                K.op(DVE, lambda e: e.tensor_tensor(out=o1[:, 0:n], in0=o1[:, 0:n], in1=rr[0][:, 0:n], op=ALU.mult), reads=[rr[0]], writes=[o1])
                                K.op(DVE, lambda e: e.tensor_tensor(out=o2[:, 0:n], in0=o2[:, 0:n], in1=rr[1][:, 0:n], op=ALU.mult), reads=[rr[1]], writes=[o2])
                                K.op(DVE, lambda e: e.tensor_tensor(out=o1[:, 0:n], in0=o1[:, 0:n], in1=o2[:, 0:n], op=ALU.add), reads=[o2], writes=[o1])
                                K.op(ACT, lambda e: e.activation(out=sqd[:, 0:n], in_=o1[:, 0:n], func=AF.Square), reads=[o1], writes=[sqd])
                                ssb = misc()
                                K.op(PE, lambda e: e.matmul(ssb[:, 0:n], onesF[:, :], sqd[:, 0:n], start=True, stop=True), reads=[onesF, sqd], writes=[ssb])
                                K.op(ACT, lambda e: e.activation(out=lnv[:, 0:n], in_=ssb[:, 0:n], func=AF.Ln, scale=1.0 / 128, bias=EPS), reads=[ssb], writes=[lnv])
                                K.op(ACT, lambda e: e.activation(out=lnv[:, 0:n], in_=lnv[:, 0:n], func=AF.Exp, scale=-0.5), writes=[lnv])
                                y = yst[cnt["y"] % 4]
                                cnt["y"] += 1
                                K.op(DVE, lambda e: e.scalar_tensor_tensor(out=y[:, 0:n], in0=o1[:, 0:n], scalar=gcs[:, 21:22],
                                                                           in1=lnv[:, 0:n], op0=ALU.mult, op1=ALU.mult),
                                     reads=[o1, gcs, lnv], writes=[y])
                                r0 = 1024 + h * 128
                                K.dma("pool", YT.t[r0:r0 + 128, q0:q0 + n], y[:, 0:n], reads=[y], partial=[YT])

                            defer(fin)
                    defer(None)

            K.barrier()
            with ExitStack() as st:
                Wg = sb(st, "Wg", [128, 8, 3072], BF16)
                for kc in range(8):
                    K.dma("pool", Wg[:, kc, :], W_IN.t[l, kc * 128:(kc + 1) * 128, N_TM:IN_COLS], reads=[W_IN],
                          **({"writes": [Wg]} if kc == 0 else {"partial": [Wg]}))
                Wup = sb(st, "Wup", [128, 12, D], BF16)
                for b in range(3):
                    K.dma("pool", Wup[:, b * 4:(b + 1) * 4, :], W_UP.t[l, b].rearrange("(c p) n -> p c n", p=128), reads=[W_UP],
                          **({"writes": [Wup]} if b == 0 else {"partial": [Wup]}))
                Wo = sb(st, "Wo", [128, 8, D], BF16)
                K.dma("pool", Wo[:], W_O.t[l].rearrange("(c p) n -> p c n", p=128), reads=[W_O], writes=[Wo])
                G1 = sb(st, "G1", [128, D], F32)
                hTc = sb(st, "hTc", [128, 8, 512], BF16, 2)
                YTc = sb(st, "YTc", [128, 12, 512], BF16, 2)
                sig = sb(st, "sig", [128, 512], F32, 3)
                acc = sb(st, "acc", [128, 512], F32, 2)
                mT = sb(st, "mT", [128, 8, 512], BF16)
                xt1 = sb(st, "xt1", [128, D], F32, 2)
                xo1 = sb(st, "xo1", [128, D], F32, 2)
                cur_ver = None
                xi = 0
                for ci, ch in enumerate(chunks):
                    ver = 1 if ch[0] < NCTX_T else 0
                    if ver != cur_ver:
                        K.dma("sp", G1[:], MODR.t[l, ver:ver + 1, 2 * D:3 * D].partition_broadcast(128), reads=[MODR], writes=[G1])
                        cur_ver = ver
                    q0, n = ch[0] * 128, len(ch) * 128
                    h_c, y_c = hTc[ci % 2], YTc[ci % 2]
                    K.dma("sp", h_c[:, :, 0:n], HT.t[:, :, q0:q0 + n].rearrange("k p t -> p k t"), reads=[HT], writes=[h_c])
                    K.dma("sp", y_c[:, :, 0:n], YT.t[:, q0:q0 + n].rearrange("(c p) t -> p c t", p=128), reads=[YT], writes=[y_c])
                    for fc in range(8):
                        for b in range(3):
                            pg = ps()
                            for kc in range(8):
                                K.op(PE, lambda e, kc=kc, pg=pg, b=b: e.matmul(pg[:, 0:n], Wg[:, kc, b * D + fc * 128:b * D + (fc + 1) * 128], h_c[:, kc, 0:n],
                                                                              start=(kc == 0), stop=(kc == 7)),
                                     reads=[Wg, h_c], **({"writes": [pg]} if kc == 0 else {"partial": [pg]}))
                            pu = ps()
                            for c in range(4):
                                K.op(PE, lambda e, c=c, pu=pu, b=b: e.matmul(pu[:, 0:n], Wup[:, b * 4 + c, fc * 128:(fc + 1) * 128], y_c[:, b * 4 + c, 0:n],
                                                                            start=(c == 0), stop=(c == 3)),
                                     reads=[Wup, y_c], **({"writes": [pu]} if c == 0 else {"partial": [pu]}))
                            sg = sig[b]
                            K.op(ACT, lambda e, sg=sg, pg=pg: e.activation(out=sg[:, 0:n], in_=pg[:, 0:n], func=AF.Sigmoid), reads=[pg], writes=[sg])
                            if b == 0:
                                K.op(DVE, lambda e, sg=sg, pu=pu: e.tensor_tensor(out=acc[0][:, 0:n], in0=sg[:, 0:n], in1=pu[:, 0:n], op=ALU.mult),
                                     reads=[sg, pu], writes=[acc[0]])
                            else:
                                K.op(DVE, lambda e, sg=sg, pu=pu: e.tensor_tensor(out=acc[1][:, 0:n], in0=sg[:, 0:n], in1=pu[:, 0:n], op=ALU.mult),
                                     reads=[sg, pu], writes=[acc[1]])
                                if b == 1:
                                    K.op(DVE, lambda e: e.tensor_tensor(out=acc[0][:, 0:n], in0=acc[0][:, 0:n], in1=acc[1][:, 0:n], op=ALU.add),
                                         reads=[acc[1]], writes=[acc[0]])
                                else:
                                    K.op(DVE, lambda e, fc=fc: e.tensor_tensor(out=mT[:, fc, 0:n], in0=acc[0][:, 0:n], in1=acc[1][:, 0:n], op=ALU.add),
                                         reads=[acc[0], acc[1]], **({"writes": [mT]} if fc == 0 else {"partial": [mT]}))
                    for j, t in enumerate(ch):
                        x_t, x_o = xt1[xi % 2], xo1[xi % 2]
                        xi += 1
                        K.dma("sp", x_t[:], XSRC.t[t * 128:(t + 1) * 128, :], reads=[XSRC], writes=[x_t])
                        for cb in range(2):
                            p = ps()
                            for kc in range(8):
                                K.op(PE, lambda e, kc=kc, p=p, j=j, cb=cb: e.matmul(p[:, :], mT[:, kc, j * 128:(j + 1) * 128], Wo[:, kc, cb * 512:(cb + 1) * 512],
                                                                                   start=(kc == 0), stop=(kc == 7)),
                                     reads=[mT, Wo], **({"writes": [p]} if kc == 0 else {"partial": [p]}))
                            K.op(DVE, lambda e, p=p, cb=cb: e.tensor_tensor(out=acc[0][:, :], in0=p[:, :], in1=G1[:, cb * 512:(cb + 1) * 512], op=ALU.mult),
                                 reads=[p, G1], writes=[acc[0]])
                            K.op(DVE, lambda e, cb=cb, x_t=x_t, x_o=x_o: e.tensor_tensor(out=x_o[:, cb * 512:(cb + 1) * 512], in0=acc[0][:, :],
                                                                                        in1=x_t[:, cb * 512:(cb + 1) * 512], op=ALU.add),
                                 reads=[acc[0], x_t], **({"writes": [x_o]} if cb == 0 else {"partial": [x_o]}))
                        K.dma("pool", XM.t[t * 128:(t + 1) * 128, :], x_o[:], reads=[x_o], partial=[XM])

            K.barrier()
            with ExitStack() as st:
                A2, B2, gc2 = mod_cols(st, 3, 4, 8)
                Wmi = sb(st, "Wmi", [128, 8, 4 * D], BF16)
                for kc in range(8):
                    K.dma("pool", Wmi[:, kc, :], W_MI.t[l, kc * 128:(kc + 1) * 128, :], reads=[W_MI],
                          **({"writes": [Wmi]} if kc == 0 else {"partial": [Wmi]}))
                Wmo = sb(st, "Wmo", [128, 32, D], BF16)
                for c4 in range(4):
                    K.dma("pool", Wmo[:, c4 * 8:(c4 + 1) * 8, :], W_MO.t[l, c4 * D:(c4 + 1) * D, :].rearrange("(c p) n -> p c n", p=128), reads=[W_MO],
                          **({"writes": [Wmo]} if c4 == 0 else {"partial": [Wmo]}))
                G2 = sb(st, "G2", [128, D], F32)
                xt2 = sb(st, "xt2", [128, 2, D], F32, 2)
                junk2 = sb(st, "junk2", [128, D], BF16)
                ssq2 = sb(st, "ssq2", [128, 4], F32, 2)
                xn2 = sb(st, "xn2", [128, D], F32)
                h2T = sb(st, "h2T", [128, 8, 256], BF16)
                hidT = sb(st, "hidT", [128, 32, 256], BF16)
                rr = sb(st, "rr", [128, 256], F32, 2)
                xo2 = sb(st, "xo2", [128, D], F32, 2)
                tmp2 = sb(st, "tmp2", [128, 512], F32)
                chunks2 = []
                for ch in chunks:
                    for i in range(0, len(ch), 2):
                        chunks2.append(ch[i:i + 2])
                cur_ver = None
                xi = 0
                for ci, ch in enumerate(chunks2):
                    ver = 1 if ch[0] < NCTX_T else 0
                    if ver != cur_ver:
                        K.dma("sp", G2[:], MODR.t[l, ver:ver + 1, 5 * D:6 * D].partition_broadcast(128), reads=[MODR], writes=[G2])
                        cur_ver = ver
                    n = len(ch) * 128
                    x_c = xt2[ci % 2]
                    for j, t in enumerate(ch):
                        K.dma("sp", x_c[:, j, :], XM.t[t * 128:(t + 1) * 128, :], reads=[XM], **({"writes": [x_c]} if j == 0 else {"partial": [x_c]}))
                    for j, t in enumerate(ch):
                        norm_transpose(x_c[:, j, :], x_c, junk2, ssq2[j], xn2, A2, B2, ver,
                                       lambda kc, j=j: h2T[:, kc, j * 128:(j + 1) * 128], h2T, j == 0)
                    for fc in range(32):
                        p = ps()
                        for kc in range(8):
                            K.op(PE, lambda e, kc=kc, p=p, fc=fc: e.matmul(p[:, 0:n], Wmi[:, kc, fc * 128:(fc + 1) * 128], h2T[:, kc, 0:n], start=(kc == 0), stop=(kc == 7)),
                                 reads=[Wmi, h2T], **({"writes": [p]} if kc == 0 else {"partial": [p]}))
                        r_t = rr[fc % 2]
                        K.op(ACT, lambda e, p=p, r_t=r_t: e.activation(out=r_t[:, 0:n], in_=p[:, 0:n], func=AF.Relu), reads=[p], writes=[r_t])
                        K.op(DVE, lambda e, r_t=r_t, fc=fc: e.tensor_tensor(out=hidT[:, fc, 0:n], in0=r_t[:, 0:n], in1=r_t[:, 0:n], op=ALU.mult),
                             reads=[r_t], **({"writes": [hidT]} if fc == 0 else {"partial": [hidT]}))
                    for j, t in enumerate(ch):
                        x_o = xo2[xi % 2]
                        xi += 1
                        for cb in range(2):
                            p = ps()
                            for fc in range(32):
                                K.op(PE, lambda e, fc=fc, p=p, j=j, cb=cb: e.matmul(p[:, :], hidT[:, fc, j * 128:(j + 1) * 128], Wmo[:, fc, cb * 512:(cb + 1) * 512],
                                                                                   start=(fc == 0), stop=(fc == 31)),
                                     reads=[hidT, Wmo], **({"writes": [p]} if fc == 0 else {"partial": [p]}))
                            K.op(DVE, lambda e, p=p, cb=cb: e.tensor_tensor(out=tmp2[:, :], in0=p[:, :], in1=G2[:, cb * 512:(cb + 1) * 512], op=ALU.mult),
                                 reads=[p, G2], writes=[tmp2])
                            K.op(DVE, lambda e, cb=cb, j=j, x_o=x_o, x_c=x_c: e.tensor_tensor(out=x_o[:, cb * 512:(cb + 1) * 512], in0=tmp2[:, :],
                                                                                             in1=x_c[:, j, cb * 512:(cb + 1) * 512], op=ALU.add),
                                 reads=[tmp2, x_c], **({"writes": [x_o]} if cb == 0 else {"partial": [x_o]}))
                        r = dst_rows(t)
                        K.dma("pool", XDST.t[r * 128:(r + 1) * 128, :], x_o[:], reads=[x_o], partial=[XDST])
            K.barrier()
        K.finish()
    return nc


def _rope_tables(pos):
    row = (pos // GRID_W).astype(np.float32)
    col = (pos % GRID_W).astype(np.float32)
    out = []
    for rot in (64, 32):
        half = rot // 2
        freqs = (np.float32(10000.0) ** (-np.arange(0, half, 2, dtype=np.float32) / np.float32(half))).astype(np.float32)
        ar = (row[:, None] * freqs).astype(np.float32)
        ac = (col[:, None] * freqs).astype(np.float32)
        cr, sr, cc, sc = np.cos(ar), np.sin(ar), np.cos(ac), np.sin(ac)
        out.append(np.concatenate([cr, cr, cc, cc], axis=1))
        out.append(np.concatenate([-sr, sr, -sc, sc], axis=1))
    return np.concatenate(out, axis=1).astype(np.float32)


def _swap_pairs(g, q4):
    return g.reshape(2, 2, q4)[:, ::-1, :].reshape(-1)


_PROG_CACHE = {}


def kernel(**inputs):
    inp = {k: np.asarray(v) for k, v in inputs.items()}
    x = inp["x"].astype(np.float32, copy=False)
    B, L, _ = x.shape
    NLT = L // 128
    NH = NLT // 2
    n_cores = 2 * B
    if NLT not in _PROG_CACHE:
        _PROG_CACHE[NLT] = build_program(NLT)
    nc = _PROG_CACHE[NLT]

    f32 = np.float32
    w_up = np.stack([inp["w_up_mla"], inp["w_up_swa"], inp["w_up_diff"]], axis=1).astype(f32, copy=False)
    gcol = np.zeros((DEPTH, 128, 24), f32)
    grow = np.zeros((DEPTH, 12, 96), f32)
    for l in range(DEPTH):
        gcol[l, :, 0:8] = inp["g_norm_attn"][l].reshape(8, 128).T
        gcol[l, :, 8:16] = inp["g_norm_mlp"][l].reshape(8, 128).T
        gcol[l, :, 16:18] = inp["g_q_lora"][l].reshape(2, 128).T
        gcol[l, :, 18] = inp["g_kv_lora"][l]
        gcol[l, 0:64, 19] = inp["g_diff_sub"][l][0:64]
        gcol[l, 0:64, 20] = inp["g_diff_sub"][l][64:128]
        gcol[l, :, 21] = inp["g_diff_sub"][l]
        for gi, name in enumerate(("g_mla_q", "g_mla_k")):
            g = inp[name][l]
            grow[l, 2 * gi, :] = g
            grow[l, 2 * gi + 1, 64:96] = _swap_pairs(g[64:96], 8)
        for gi, name in enumerate(("g_swa_q", "g_swa_k", "g_diff_q", "g_diff_k")):
            g = inp[name][l]
            grow[l, 4 + 2 * gi, 0:64] = g
            grow[l, 5 + 2 * gi, 0:64] = _swap_pairs(g, 16)
    lamv = np.concatenate([inp["lambda_q1"], inp["lambda_k1"], inp["lambda_q2"], inp["lambda_k2"]], axis=1).astype(f32)
    ii = np.arange(128)
    masks = np.zeros((128, 2, 128), f32)
    masks[:, 0, :] = (ii[:, None] >= ii[None, :])
    masks[:, 1, :] = (ii[:, None] <= ii[None, :])
    identf = np.eye(128, dtype=f32)
    shared = {
        "identf": identf, "masks": masks,
        "w_mod": inp["w_mod"], "b_mod": inp["b_mod"], "gcol": gcol, "grow": grow.reshape(DEPTH, 12 * 96),
        "lamv": lamv, "sink": inp["swa_sink"], "w_in": inp["w_in"], "w_uq": inp["w_uq"], "w_ukv": inp["w_ukv"],
        "w_up": w_up, "w_o": inp["w_o"], "w_mlp_in": inp["w_mlp_in"], "w_mlp_out": inp["w_mlp_out"],
    }
    shared = {k: np.ascontiguousarray(v, dtype=f32) for k, v in shared.items()}
    in_maps = []
    for c in range(n_cores):
        b, s = c // 2, c % 2
        blocks = list(range(s * NH, (s + 1) * NH)) + list(range((1 - s) * NH, (2 - s) * NH))
        tok = np.concatenate([np.arange(bk * 128, (bk + 1) * 128) for bk in blocks])
        xin = np.concatenate([inp["ctx"][b], x[b][tok]], axis=0).astype(f32, copy=False)
        rope = np.zeros((256 + L, 192), f32)
        rope[0:256, 0:64] = 1.0
        rope[0:256, 128:160] = 1.0
        rope[256:] = _rope_tables(tok)
        cT = np.zeros((128, 16), f32)
        cT[:, 0::2] = inp["c"][b].reshape(8, 128).T
        cT[:, 1::2] = inp["c_ctx"].reshape(8, 128).T
        edge = np.zeros((128, 4), f32)
        edge[:] = [0, 0, 1, 1] if s == 0 else [1, 1, 0, 0]
        m = dict(shared)
        m.update({"xin": np.ascontiguousarray(xin), "rope": rope, "cT": cT, "edge": edge})
        in_maps.append(m)
    res = run_bass_kernel_spmd(nc, in_maps, core_ids=list(range(n_cores)))
    if DEBUG:
        DEBUG_RES.append(res.results)
    out = np.empty((B, L, D), f32)
    for c in range(n_cores):
        b, s = c // 2, c % 2
        out[b, s * NH * 128:(s + 1) * NH * 128, :] = res.results[c]["out"]
    return out
```

```python
import math
from contextlib import ExitStack

import numpy as np
import ml_dtypes

import concourse.bass as bass
import concourse.mybir as mybir
from concourse.bass_utils import run_bass_kernel_spmd

F32 = mybir.dt.float32
BF16 = mybir.dt.bfloat16
AF = mybir.ActivationFunctionType
ALU = mybir.AluOpType
AX = mybir.AxisListType

D = 1024
DEPTH = 2
GRID_W = 64
NCTX_T = 2
EPS = 1e-6
MLA_SCALE = 96 ** -0.5
HD_SCALE = 64 ** -0.5
IN_COLS = 5792
N_TM = 2720
NDS = 12
DEBUG = False
NLAYERS = DEPTH
DEBUG_RES = []


class Buf:
    __slots__ = ("w", "r")

    def __init__(self):
        self.w = {}
        self.r = {}


class Tile:
    def __init__(self, t):
        self.t = t
        self.b = Buf()

    def __getitem__(self, k):
        return self.t[k]


def _b(x):
    return x.b if hasattr(x, "b") else x


class Eng:
    def __init__(self, eng, sem_idx, strict=True):
        self.eng = eng
        self.sem = sem_idx
        self.n = 0
        self.waited = {}
        self.strict = strict


class DmaQ:
    def __init__(self, E, sems):
        self.E = E
        self.sems = sems
        self.cnt = [0] * len(sems)
        self.i = 0


class KB:
    def __init__(self, nc, es):
        self.nc = nc
        self.sems = []

        def mk(name):
            self.sems.append(es.enter_context(nc.semaphore(name)))
            return len(self.sems) - 1

        self.PE = Eng(nc.tensor, mk("s_pe"), strict=False)
        self.ACT = Eng(nc.scalar, mk("s_act"))
        self.DVE = Eng(nc.vector, mk("s_dve"))
        self.POOL = Eng(nc.gpsimd, mk("s_pool"))
        self.SP = Eng(nc.sync, None)
        self.q = {
            "sp": DmaQ(self.SP, [mk("d_sp%d" % i) for i in range(NDS)]),
            "pool": DmaQ(self.POOL, [mk("d_pl%d" % i) for i in range(NDS)]),
        }
        self.ninst = 0

    def _deps(self, reads, writes, partial):
        deps = {}

        def add(d):
            for s, v in d.items():
                if deps.get(s, 0) < v:
                    deps[s] = v

        for x in reads:
            add(_b(x).w)
        for x in writes:
            b = _b(x)
            add(b.w)
            add(b.r)
        for x in partial:
            add(_b(x).r)
        return deps

    def _wait(self, E, deps):
        for s, v in deps.items():
            if E.waited.get(s, 0) >= v:
                continue
            E.eng.wait_ge(self.sems[s], v)
            E.waited[s] = v

    @staticmethod
    def _mark(reads, writes, partial, s, v):
        for x in reads:
            b = _b(x)
            if b.r.get(s, 0) < v:
                b.r[s] = v
        for x in list(writes) + list(partial):
            b = _b(x)
            if b.w.get(s, 0) < v:
                b.w[s] = v

    def op(self, E, fn, reads=(), writes=(), partial=()):
        deps = self._deps(reads, writes, partial)
        if not E.strict:
            deps.pop(E.sem, None)
        self._wait(E, deps)
        ins = fn(E.eng)
        E.n += 1
        ins.then_inc(self.sems[E.sem], 1)
        self._mark(reads, writes, partial, E.sem, E.n)
        self.ninst += 1

    def dma(self, qn, out, in_, reads=(), writes=(), partial=(), slow=False):
        q = self.q[qn]
        E = q.E
        k = q.i % len(q.sems)
        q.i += 1
        s = q.sems[k]
        deps = self._deps(reads, writes, partial)
        if q.cnt[k] > 0:
            deps[s] = max(deps.get(s, 0), 16 * q.cnt[k])
        self._wait(E, deps)
        if slow:
            ins = E.eng.dma_start(out=out, in_=in_, allow_slow_non_contiguous=True)
        else:
            ins = E.eng.dma_start(out=out, in_=in_)
        q.cnt[k] += 1
        ins.then_inc(self.sems[s], 16)
        self._mark(reads, writes, partial, s, 16 * q.cnt[k])
        self.ninst += 1

    def barrier(self):
        deps = {}
        for q in self.q.values():
            for k, sm in enumerate(q.sems):
                if q.cnt[k]:
                    deps[sm] = 16 * q.cnt[k]
        for E in (self.PE, self.ACT, self.DVE, self.POOL):
            if E.n:
                deps[E.sem] = E.n
        for E in (self.PE, self.ACT, self.DVE, self.POOL, self.SP):
            self._wait(E, dict(deps))

    def finish(self):
        deps = {}
        for q in self.q.values():
            for k, s in enumerate(q.sems):
                if q.cnt[k]:
                    deps[s] = 16 * q.cnt[k]
        for E in (self.PE, self.ACT, self.DVE, self.POOL):
            if E.n:
                deps[E.sem] = E.n
        self._wait(self.SP, deps)


def build_program(NLT):
    TT = NLT + NCTX_T
    T = TT * 128
    NOWN = NLT // 2
    nc = bass.Bass("TRN2", target_bir_lowering=False)

    def din(name, shape, dt=F32):
        return Tile(nc.dram_tensor(name, list(shape), dt, kind="ExternalInput").ap())

    def dscr(name, shape, dt=BF16):
        return Tile(nc.dram_tensor(name, list(shape), dt, kind=("ExternalOutput" if DEBUG else "Internal")).ap())

    XIN = din("xin", [T, D])
    CT = din("cT", [128, 16])
    ROPE = din("rope", [T, 192])
    IDF = din("identf", [128, 128])
    MASKS = din("masks", [128, 2, 128])
    EDGE = din("edge", [128, 4])
    W_MOD = din("w_mod", [DEPTH, D, 6 * D])
    B_MOD = din("b_mod", [DEPTH, 6 * D])
    GCOL = din("gcol", [DEPTH, 128, 24])
    GROW = din("grow", [DEPTH, 12 * 96])
    LAMV = din("lamv", [DEPTH, 4 * 64])
    SINK = din("sink", [DEPTH, 8])
    W_IN = din("w_in", [DEPTH, D, IN_COLS])
    W_UQ = din("w_uq", [DEPTH, 256, 768])
    W_UKV = din("w_ukv", [DEPTH, 128, 1024])
    W_UP = din("w_up", [DEPTH, 3, 512, D])
    W_O = din("w_o", [DEPTH, D, D])
    W_MI = din("w_mlp_in", [DEPTH, D, 4 * D])
    W_MO = din("w_mlp_out", [DEPTH, 4 * D, D])
    OUT = Tile(nc.dram_tensor("out", [NOWN * 128, D], F32, kind="ExternalOutput").ap())

    X1 = dscr("x1", [T, D], F32)
    XM = dscr("xm", [T, D], F32)
    MODR = dscr("modr", [DEPTH, 2, 6 * D], F32)
    HT = dscr("ht", [8, 128, T])
    QTm = dscr("qtm", [8, 96, T])
    KTm = dscr("ktm", [8, 96, T])
    Vm = dscr("vm", [T, 512])
    QTs = dscr("qts", [8, 128, T])
    KTs = dscr("kts", [128, T])
    Vs = dscr("vs", [T, 128])
    QTd = dscr("qtd", [8, 128, T])
    KTd = dscr("ktd", [4, 128, T])
    Vd = dscr("vd", [T, 512])
    YT = dscr("yt", [1536, T])

    es = ExitStack()
    with es:
        K = KB(nc, es)
        PE, ACT, DVE, POOL = K.PE, K.ACT, K.DVE, K.POOL

        uid = [0]

        def sb(st, name, shape, dt, n=1):
            uid[0] += 1
            tl = [Tile(st.enter_context(nc.sbuf_tensor("%s_u%d_%d" % (name, uid[0], i), list(shape), dt))) for i in range(n)]
            return tl if n > 1 else tl[0]

        PS = [Tile(es.enter_context(nc.psum_tensor("ps%d" % i, [128, 512], F32))) for i in range(8)]
        ps_i = [0]

        ps_pool = [list(PS)]
        tr_pool = [PS[5:8]]
        tr_i = [0]

        def ps():
            pool = ps_pool[0]
            p = pool[ps_i[0] % len(pool)]
            ps_i[0] += 1
            return p

        def ps_tr():
            pool = tr_pool[0]
            p = pool[tr_i[0] % len(pool)]
            tr_i[0] += 1
            return p

        identf = sb(es, "identf", [128, 128], F32)
        identb = sb(es, "identb", [128, 128], BF16)
        selF = sb(es, "selF", [128, 128], F32)
        ones64 = sb(es, "ones64", [64, 128], F32)
        masks = sb(es, "masks", [128, 6, 128], BF16)
        edge = sb(es, "edge", [128, 4], F32)
        scT = sb(es, "scT", [128, 16], F32)
        K.dma("sp", identf[:], IDF[:], reads=[IDF], writes=[identf])
        K.dma("pool", identb[:], IDF[:], reads=[IDF], writes=[identb])
        K.dma("pool", masks[:, 0:2, :], MASKS[:], reads=[MASKS], writes=[masks])
        K.dma("sp", edge[:], EDGE[:], reads=[EDGE], writes=[edge])
        K.dma("sp", scT[:], CT[:], reads=[CT], writes=[scT])
        K.op(DVE, lambda e: e.memset(selF[:], 0.0), writes=[selF])
        K.op(DVE, lambda e: e.memset(selF[64:65, :], 1.0), partial=[selF])
        K.op(DVE, lambda e: e.memset(ones64[:], 1.0), writes=[ones64])
        for i, src in ((2, 0), (3, 1), (4, 0), (5, 1)):
            K.op(DVE, lambda e, i=i, src=src: e.tensor_scalar(masks[:, i, :], masks[:, src, :], edge[:, i - 2:i - 1], None, ALU.mult),
                 reads=[edge, masks], partial=[masks])
        K.op(ACT, lambda e: e.activation(out=scT[:], in_=scT[:], func=AF.Silu), writes=[scT])

        def swa_mask(j, side):
            if side == 0:
                if j == 0:
                    return 2
                if j == NLT // 2:
                    return 4
                return 0
            if j == NLT - 1:
                return 3
            if j == NLT // 2 - 1:
                return 5
            return 1

        for l in range(NLAYERS):
            last = l == DEPTH - 1
            XSRC = XIN if l == 0 else X1
            XDST = OUT if last else X1
            if last:
                q_tiles = list(range(NCTX_T, NCTX_T + NOWN))
            else:
                q_tiles = list(range(TT))
            q_set = set(q_tiles)
            kv_tiles = list(range(TT))
            chunks = []
            if 0 in q_set:
                chunks.append([0, 1])
            lat_q = [t for t in q_tiles if t >= NCTX_T]
            for i in range(0, len(lat_q), 4):
                chunks.append(lat_q[i:i + 4])

            def dst_rows(t):
                return (t - NCTX_T) if last else t

            with ExitStack() as st:
                wm = sb(st, "wm", [128, 8, 512], F32, 2)
                modrow = sb(st, "modrow", [2, 6 * D], F32)
                bmod2 = sb(st, "bmod2", [2, 6 * D], F32)
                K.dma("sp", bmod2[:], B_MOD.t[l:l + 1, :].partition_broadcast(2),
                      reads=[B_MOD], writes=[bmod2])
                for blk in range(12):
                    w = wm[blk % 2]
                    K.dma("sp", w[:], W_MOD.t[l, :, blk * 512:(blk + 1) * 512].rearrange("(k p) c -> p k c", p=128),
                          reads=[W_MOD], writes=[w])
                    p = ps()
                    for kc in range(8):
                        K.op(PE, lambda e, kc=kc, p=p, w=w: e.matmul(p[0:2, :], scT[:, 2 * kc:2 * kc + 2], w[:, kc, :],
                                                                  start=(kc == 0), stop=(kc == 7)),
                             reads=[scT, w], **({"writes": [p]} if kc == 0 else {"partial": [p]}))
                    K.op(DVE, lambda e, p=p, blk=blk: e.tensor_tensor(out=modrow[:, blk * 512:(blk + 1) * 512], in0=p[0:2, :],
                                                                     in1=bmod2[:, blk * 512:(blk + 1) * 512], op=ALU.add),
                         reads=[p, bmod2], **({"writes": [modrow]} if blk == 0 else {"partial": [modrow]}))
                K.dma("pool", MODR.t[l], modrow[:], reads=[modrow], writes=[MODR])

            K.barrier()

            def load_cols(dst, vi):
                for ver in range(2):
                    K.dma("sp", dst[:, :, ver], MODR.t[l, ver, vi * D:(vi + 1) * D].rearrange("(k p) -> p k", p=128),
                          reads=[MODR], **({"writes": [dst]} if ver == 0 else {"partial": [dst]}), slow=True)

            def mod_cols(st, vi_shift, vi_scale, gcol0):
                A = sb(st, "colA", [128, 8, 2], F32)
                Bc = sb(st, "colB", [128, 8, 2], F32)
                gc = sb(st, "gcolt", [128, 24], F32)
                K.dma("sp", gc[:], GCOL.t[l], reads=[GCOL], writes=[gc])
                load_cols(A, vi_scale)
                load_cols(Bc, vi_shift)
                K.op(DVE, lambda e: e.tensor_scalar(A[:], A[:], 1.0, None, ALU.add), writes=[A])
                K.op(DVE, lambda e: e.tensor_tensor(out=A[:], in0=A[:], in1=gc[:, gcol0:gcol0 + 8].unsqueeze(2).to_broadcast([128, 8, 2]),
                                                    op=ALU.mult), reads=[gc], writes=[A])
                return A, Bc, gc

            def norm_transpose(x_ap, x_tile, junk, ssq, xn, A, Bc, ver, dst_fn, dst_tile, first_write):
                K.op(ACT, lambda e: e.activation(out=junk[:], in_=x_ap, func=AF.Square, accum_out=ssq[:, 0:1]),
                     reads=[x_tile], writes=[junk, ssq])
                K.op(DVE, lambda e: e.tensor_scalar(ssq[:, 1:2], ssq[:, 0:1], 1.0 / D, EPS, ALU.mult, ALU.add), writes=[ssq])
                K.op(ACT, lambda e: e.activation(out=ssq[:, 2:3], in_=ssq[:, 1:2], func=AF.Sqrt), writes=[ssq])
                K.op(DVE, lambda e: e.reciprocal(ssq[:, 3:4], ssq[:, 2:3]), writes=[ssq])
                K.op(DVE, lambda e: e.tensor_scalar(xn[:], x_ap, ssq[:, 3:4], None, ALU.mult), reads=[x_tile, ssq], writes=[xn])
                for half in range(2):
                    p = ps()
                    for j in range(4):
                        kc = half * 4 + j
                        K.op(PE, lambda e, p=p, j=j, kc=kc: e.transpose(p[:, j * 128:(j + 1) * 128], xn[:, kc * 128:(kc + 1) * 128], identf[:]),
                             reads=[xn, identf], **({"writes": [p]} if j == 0 else {"partial": [p]}))
                    for j in range(4):
                        kc = half * 4 + j
                        fw = first_write and kc == 0
                        K.op(ACT, lambda e, p=p, j=j, kc=kc: e.activation(out=dst_fn(kc), in_=p[:, j * 128:(j + 1) * 128], func=AF.Identity,
                                                                         scale=A[:, kc, ver:ver + 1], bias=Bc[:, kc, ver:ver + 1]),
                             reads=[p, A, Bc], **({"writes": [dst_tile]} if fw else {"partial": [dst_tile]}))

            with ExitStack() as st:
                ps_pool[0] = PS[0:5]
                A1, B1, gc = mod_cols(st, 0, 1, 0)
                Win = sb(st, "Win", [128, 8, N_TM], BF16)
                for kc in range(8):
                    K.dma("pool", Win[:, kc, :], W_IN.t[l, kc * 128:(kc + 1) * 128, 0:N_TM], reads=[W_IN],
                          **({"writes": [Win]} if kc == 0 else {"partial": [Win]}))
                wuq_f = sb(st, "wuq_f", [128, 2, 768], F32)
                wukv_f = sb(st, "wukv_f", [128, 1024], F32)
                wuq = sb(st, "wuq", [128, 2, 768], BF16)
                wukv = sb(st, "wukv", [128, 1024], BF16)
                K.dma("sp", wuq_f[:], W_UQ.t[l].rearrange("(c p) n -> p c n", p=128), reads=[W_UQ], writes=[wuq_f])
                K.dma("sp", wukv_f[:], W_UKV.t[l], reads=[W_UKV], writes=[wukv_f])
                for c in range(2):
                    K.op(DVE, lambda e, c=c: e.tensor_scalar(wuq[:, c, :], wuq_f[:, c, :], gc[:, 16 + c:17 + c], None, ALU.mult),
                         reads=[wuq_f, gc], **({"writes": [wuq]} if c == 0 else {"partial": [wuq]}))
                K.op(DVE, lambda e: e.tensor_scalar(wukv[:], wukv_f[:], gc[:, 18:19], None, ALU.mult), reads=[wukv_f, gc], writes=[wukv])
                gbc = sb(st, "gbc", [128, 12, 96], F32)
                K.dma("sp", gbc[:].rearrange("p g d -> p (g d)"), GROW.t[l:l + 1, :].partition_broadcast(128),
                      reads=[GROW], writes=[gbc])

                xt = sb(st, "xt", [128, D], F32, 2)
                rp = sb(st, "rp", [128, 192], F32, 2)
                junk = sb(st, "junk", [128, D], BF16)
                ssq = sb(st, "ssq", [128, 4], F32, 2)
                xn = sb(st, "xn", [128, D], F32)
                hT = sb(st, "hT", [128, 8, 128], BF16, 2)
                latT = sb(st, "latT", [128, 3, 128], BF16)
                st8 = sb(st, "st8", [128, 16], F32, 2)
                kpe = sb(st, "kpe", [128, 32], F32)
                tabs = sb(st, "tabs", [128, 6, 2, 96], F32, 4)
                xsrc = sb(st, "xsrc", [128, 768], F32, 4)
                grp = sb(st, "grp", [128, 1664], F32, 2)
                sqv = sb(st, "sqv", [128, 768], F32, 3)
                hs = sb(st, "hs", [128, 8, 4], F32, 12)
                xnn = sb(st, "xnn", [128, 768], F32, 3)
                t1 = sb(st, "t1", [128, 768], F32, 3)
                t2 = sb(st, "t2", [128, 512], F32, 3)
                MQ = sb(st, "MQ", [128, 8, 96], BF16, 2)
                MK = sb(st, "MK", [128, 8, 96], BF16, 2)
                SQp = sb(st, "SQp", [128, 8, 128], BF16, 2)
                DQp = sb(st, "DQp", [128, 8, 128], BF16, 2)
                SKt = sb(st, "SKt", [128, 128], BF16, 2)
                DKt = sb(st, "DKt", [128, 512], BF16, 2)
                Vm_s = sb(st, "Vm_s", [128, 8, 64], BF16, 2)
                Vs_s = sb(st, "Vs_s", [128, 128], BF16, 2)
                Vd_s = sb(st, "Vd_s", [128, 512], BF16, 2)
                o_qtm = sb(st, "o_qtm", [128, 8, 128], BF16, 2)
                o_ktm = sb(st, "o_ktm", [128, 8, 128], BF16, 2)
                o_qts = sb(st, "o_qts", [128, 8, 128], BF16, 2)
                o_qtd = sb(st, "o_qtd", [128, 8, 128], BF16, 2)
                o_kt = sb(st, "o_kt", [128, 5, 128], BF16, 2)
                for i in range(2):
                    K.op(POOL, lambda e, i=i: e.memset(SQp[i][:], 0.0), writes=[SQp[i]])
                    K.op(POOL, lambda e, i=i: e.memset(DQp[i][:], 0.0), writes=[DQp[i]])

                def hnr(it, k, src_ap, src_tiles, H, d, gi, pieces, dst_tile, rope_lo):
                    n = H * d
                    hst = hs[(it % 2) * 6 + gi]
                    tb = tabs[it % 4]
                    sqv_, xnn_, t1_, t2_ = sqv[k], xnn[k], t1[k], t2[k]
                    K.op(ACT, lambda e: e.activation(out=sqv_[:, 0:n], in_=src_ap, func=AF.Square), reads=src_tiles, writes=[sqv_])
                    yield
                    K.op(DVE, lambda e: e.tensor_reduce(out=hst[:, 0:H, 0], in_=sqv_[:, 0:n].rearrange("p (h d) -> p h d", h=H), axis=AX.X, op=ALU.add),
                         reads=[sqv_], writes=[hst])
                    yield
                    K.op(DVE, lambda e: e.tensor_scalar(hst[:, 0:H, 1], hst[:, 0:H, 0], 1.0 / d, EPS, ALU.mult, ALU.add), writes=[hst])
                    yield
                    K.op(ACT, lambda e: e.activation(out=hst[:, 0:H, 2], in_=hst[:, 0:H, 1], func=AF.Sqrt), writes=[hst])
                    yield
                    K.op(DVE, lambda e: e.reciprocal(hst[:, 0:H, 3], hst[:, 0:H, 2]), writes=[hst])
                    yield
                    K.op(DVE, lambda e: e.tensor_tensor(out=xnn_[:, 0:n].rearrange("p (h d) -> p h d", h=H),
                                                        in0=src_ap.rearrange("p (h d) -> p h d", h=H),
                                                        in1=hst[:, 0:H, 3:4].to_broadcast([128, H, d]), op=ALU.mult),
                         reads=src_tiles + [hst], writes=[xnn_])
                    yield
                    xv = xnn_[:, 0:n].rearrange("p (h d) -> p h d", h=H)
                    cg = tb[:, gi, 0, 0:d].unsqueeze(1).to_broadcast([128, H, d])
                    K.op(DVE, lambda e: e.tensor_tensor(out=t1_[:, 0:n].rearrange("p (h d) -> p h d", h=H), in0=xv, in1=cg, op=ALU.mult),
                         reads=[xnn_, tb], writes=[t1_])
                    yield
                    nr = d - rope_lo
                    q4 = nr // 4
                    t2v = t2_[:, 0:H * nr].rearrange("p (h d) -> p h d", h=H)
                    for blk in range(2):
                        for hf in range(2):
                            lo_dst = rope_lo + blk * 2 * q4 + hf * q4
                            lo_src = rope_lo + blk * 2 * q4 + (1 - hf) * q4
                            first = blk == 0 and hf == 0
                            K.op(POOL if hf else DVE, lambda e, lo_dst=lo_dst, lo_src=lo_src: e.tensor_tensor(
                                out=t2v[:, :, lo_dst - rope_lo:lo_dst - rope_lo + q4],
                                in0=xv[:, :, lo_src:lo_src + q4],
                                in1=tb[:, gi, 1, lo_dst:lo_dst + q4].unsqueeze(1).to_broadcast([128, H, q4]), op=ALU.mult),
                                reads=[xnn_, tb], **({"writes": [t2_]} if first else {"partial": [t2_]}))
                            yield
                    t1v = t1_[:, 0:n].rearrange("p (h d) -> p h d", h=H)
                    first = True
                    for (hsel, dst_fn) in pieces:
                        if rope_lo > 0:
                            K.op(ACT, lambda e, hsel=hsel, dst_fn=dst_fn: e.activation(out=dst_fn(0, rope_lo), in_=t1v[:, hsel, 0:rope_lo], func=AF.Copy),
                                 reads=[t1_], **({"writes": [dst_tile]} if first else {"partial": [dst_tile]}))
                            first = False
                            yield
                        K.op(DVE, lambda e, hsel=hsel, dst_fn=dst_fn: e.tensor_tensor(out=dst_fn(rope_lo, d), in0=t1v[:, hsel, rope_lo:d],
                                                                                       in1=t2v[:, hsel, :], op=ALU.add),
                             reads=[t1_, t2_], **({"writes": [dst_tile]} if first else {"partial": [dst_tile]}))
                        first = False
                        yield

                def interleave(gens):
                    gens = list(gens)
                    while gens:
                        for g in list(gens):
                            try:
                                next(g)
                            except StopIteration:
                                gens.remove(g)

                def front1(it, t):
                    need_q = t in q_set
                    ver = 1 if t < NCTX_T else 0
                    s = it % 2
                    x_t, rp_t, hT_t = xt[s], rp[s], hT[s]
                    K.dma("sp", x_t[:], XSRC.t[t * 128:(t + 1) * 128, :], reads=[XSRC], writes=[x_t])
                    K.dma("sp", rp_t[:], ROPE.t[t * 128:(t + 1) * 128, :], reads=[ROPE], writes=[rp_t])
                    tb = tabs[it % 4]
                    first = True
                    for gi, (d, c_lo, s_lo) in enumerate(((96, 128, 160), (96, 128, 160), (64, 0, 64), (64, 0, 64), (64, 0, 64), (64, 0, 64))):
                        kw = {"writes": [tb]} if first else {"partial": [tb]}
                        first = False
                        if d == 96:
                            K.op(POOL, lambda e, gi=gi: e.tensor_copy(tb[:, gi, 0, 0:64], gbc[:, 2 * gi, 0:64]), reads=[gbc], **kw)
                            K.op(POOL, lambda e, gi=gi, c_lo=c_lo: e.tensor_tensor(out=tb[:, gi, 0, 64:96], in0=rp_t[:, c_lo:c_lo + 32],
                                                                                 in1=gbc[:, 2 * gi, 64:96], op=ALU.mult),
                                 reads=[gbc, rp_t], partial=[tb])
                            K.op(POOL, lambda e, gi=gi, s_lo=s_lo: e.tensor_tensor(out=tb[:, gi, 1, 64:96], in0=rp_t[:, s_lo:s_lo + 32],
                                                                                 in1=gbc[:, 2 * gi + 1, 64:96], op=ALU.mult),
                                 reads=[gbc, rp_t], partial=[tb])
                        else:
                            K.op(POOL, lambda e, gi=gi, c_lo=c_lo: e.tensor_tensor(out=tb[:, gi, 0, 0:64], in0=rp_t[:, c_lo:c_lo + 64],
                                                                                 in1=gbc[:, 2 * gi, 0:64], op=ALU.mult),
                                 reads=[gbc, rp_t], **kw)
                            K.op(POOL, lambda e, gi=gi, s_lo=s_lo: e.tensor_tensor(out=tb[:, gi, 1, 0:64], in0=rp_t[:, s_lo:s_lo + 64],
                                                                                 in1=gbc[:, 2 * gi + 1, 0:64], op=ALU.mult),
                                 reads=[gbc, rp_t], partial=[tb])
                    norm_transpose(x_t[:], x_t, junk, ssq[s], xn, A1, B1, ver, lambda kc: hT_t[:, kc, :], hT_t, True)
                    if need_q:
                        K.dma("pool", HT.t[:, :, t * 128:(t + 1) * 128].rearrange("k p t -> p k t"), hT_t[:], reads=[hT_t], partial=[HT])

                def front2(it, t):
                    need_q = t in q_set
                    s = it % 2
                    hT_t = hT[s]

                    def proj_tm(c0, c1):
                        p = ps()
                        for kc in range(8):
                            K.op(PE, lambda e, kc=kc, p=p: e.matmul(p[:, 0:c1 - c0], hT_t[:, kc, :], Win[:, kc, c0:c1], start=(kc == 0), stop=(kc == 7)),
                                 reads=[hT_t, Win], **({"writes": [p]} if kc == 0 else {"partial": [p]}))
                        return p

                    st_t = st8[s]
                    c_lo = 0 if need_q else 256
                    p0 = proj_tm(c_lo, 416)
                    off = -c_lo
                    K.op(DVE, lambda e: e.memset(st_t[:, 0:2], 1.0), writes=[st_t])
                    if need_q:
                        K.op(ACT, lambda e: e.activation(out=junk[:, 0:256], in_=p0[:, 0:256], func=AF.Square, accum_out=st_t[:, 0:1]),
                             reads=[p0], writes=[junk, st_t])
                    K.op(ACT, lambda e: e.activation(out=junk[:, 256:384], in_=p0[:, 256 + off:384 + off], func=AF.Square, accum_out=st_t[:, 1:2]),
                         reads=[p0], writes=[junk, st_t])
                    K.op(ACT, lambda e: e.activation(out=kpe[:], in_=p0[:, 384 + off:416 + off], func=AF.Copy), reads=[p0], writes=[kpe])
                    K.op(DVE, lambda e: e.tensor_scalar(st_t[:, 2:3], st_t[:, 0:1], 1.0 / 256, EPS, ALU.mult, ALU.add), writes=[st_t])
                    K.op(DVE, lambda e: e.tensor_scalar(st_t[:, 3:4], st_t[:, 1:2], 1.0 / 128, EPS, ALU.mult, ALU.add), writes=[st_t])
                    K.op(ACT, lambda e: e.activation(out=st_t[:, 4:6], in_=st_t[:, 2:4], func=AF.Sqrt), writes=[st_t])
                    K.op(DVE, lambda e: e.reciprocal(st_t[:, 6:8], st_t[:, 4:6]), writes=[st_t])
                    pl = ps()
                    lat_chunks = [0, 1, 2] if need_q else [2]
                    for ci, c in enumerate(lat_chunks):
                        for kc in range(8):
                            K.op(PE, lambda e, kc=kc, c=c: e.matmul(pl[:, c * 128:(c + 1) * 128], Win[:, kc, c * 128:(c + 1) * 128], hT_t[:, kc, :],
                                                                   start=(kc == 0), stop=(kc == 7)),
                                 reads=[hT_t, Win], **({"writes": [pl]} if (kc == 0 and ci == 0) else {"partial": [pl]}))
                    lo = lat_chunks[0] * 128
                    K.op(ACT, lambda e: e.activation(out=latT[:].rearrange("p c t -> p (c t)")[:, lo:384], in_=pl[:, lo:384], func=AF.Copy),
                         reads=[pl], writes=[latT])
                    g_t = grp[s]
                    first = True
                    if need_q:
                        p = proj_tm(416, 928)
                        K.op(ACT, lambda e, p=p: e.activation(out=g_t[:, 0:512], in_=p[:, 0:512], func=AF.Copy), reads=[p], writes=[g_t])
                        first = False
                    p = proj_tm(928, 1184)
                    K.op(ACT, lambda e, p=p: e.activation(out=g_t[:, 512:640], in_=p[:, 0:128], func=AF.Copy), reads=[p],
                         **({"writes": [g_t]} if first else {"partial": [g_t]}))
                    vs_t = Vs_s[s]
                    K.op(ACT, lambda e, p=p: e.activation(out=vs_t[:], in_=p[:, 128:256], func=AF.Copy), reads=[p], writes=[vs_t])
                    K.dma("pool", Vs.t[t * 128:(t + 1) * 128, :], vs_t[:], reads=[vs_t], partial=[Vs])
                    if need_q:
                        p = proj_tm(1184, 1696)
                        K.op(ACT, lambda e, p=p: e.activation(out=g_t[:, 640:1152], in_=p[:, 0:512], func=AF.Copy), reads=[p], partial=[g_t])
                    p = proj_tm(1696, 2208)
                    K.op(ACT, lambda e, p=p: e.activation(out=g_t[:, 1152:1664], in_=p[:, 0:512], func=AF.Copy), reads=[p], partial=[g_t])
                    p = proj_tm(2208, 2720)
                    vd_t = Vd_s[s]
                    K.op(ACT, lambda e, p=p: e.activation(out=vd_t[:], in_=p[:, :], func=AF.Copy), reads=[p], writes=[vd_t])
                    K.dma("pool", Vd.t[t * 128:(t + 1) * 128, :], vd_t[:], reads=[vd_t], partial=[Vd])
                    if need_q:
                        xq = xsrc[2 * s]
                        for (n0, n1) in ((0, 512), (512, 768)):
                            p = ps()
                            for c in range(2):
                                K.op(PE, lambda e, c=c, p=p, n0=n0, n1=n1: e.matmul(p[:, 0:n1 - n0], latT[:, c, :], wuq[:, c, n0:n1], start=(c == 0), stop=(c == 1)),
                                     reads=[latT, wuq], **({"writes": [p]} if c == 0 else {"partial": [p]}))
                            K.op(ACT, lambda e, p=p, n0=n0, n1=n1: e.activation(out=xq[:, n0:n1], in_=p[:, 0:n1 - n0], func=AF.Identity, scale=st_t[:, 6:7]),
                                 reads=[p, st_t], **({"writes": [xq]} if n0 == 0 else {"partial": [xq]}))
                    xk = xsrc[2 * s + 1]
                    xkv = xk[:, 0:768].rearrange("p (h d) -> p h d", h=8)
                    vm_t = Vm_s[s]
                    for hh in range(2):
                        p = ps()
                        K.op(PE, lambda e, p=p, hh=hh: e.matmul(p[:, :], latT[:, 2, :], wukv[:, hh * 512:(hh + 1) * 512], start=True, stop=True),
                             reads=[latT, wukv], writes=[p])
                        pv = p[:, :].rearrange("p (h d) -> p h d", h=4)
                        K.op(ACT, lambda e, pv=pv, hh=hh: e.activation(out=xkv[:, hh * 4:(hh + 1) * 4, 0:64], in_=pv[:, :, 0:64], func=AF.Identity, scale=st_t[:, 7:8]),
                             reads=[p, st_t], **({"writes": [xk]} if hh == 0 else {"partial": [xk]}))
                        K.op(ACT, lambda e, pv=pv, hh=hh: e.activation(out=vm_t[:, hh * 4:(hh + 1) * 4, :], in_=pv[:, :, 64:128], func=AF.Identity, scale=st_t[:, 7:8]),
                             reads=[p, st_t], **({"writes": [vm_t]} if hh == 0 else {"partial": [vm_t]}))
                    K.op(POOL, lambda e: e.tensor_copy(xkv[:, :, 64:96], kpe[:].unsqueeze(1).to_broadcast([128, 8, 32])), reads=[kpe], partial=[xk])
                    K.dma("pool", Vm.t[t * 128:(t + 1) * 128, :], vm_t[:].rearrange("p h d -> p (h d)"), reads=[vm_t], partial=[Vm])

                def back_h(it, t):
                    need_q = t in q_set
                    s = it % 2
                    g_t = grp[s]
                    xq, xk = xsrc[2 * s], xsrc[2 * s + 1]
                    mq_t, mk_t, sq_t, sk_t, dq_t, dk_t = MQ[s], MK[s], SQp[s], SKt[s], DQp[s], DKt[s]
                    dqv = dq_t[:].rearrange("p (h j) c -> p h j c", j=2)
                    g_mk = lambda k: hnr(it, k, xk[:, 0:768], [xk], 8, 96, 1, [(slice(0, 8), lambda a, b_: mk_t[:, :, a:b_])], mk_t, 64)
                    g_sk = lambda k: hnr(it, k, g_t[:, 512:640], [g_t], 2, 64, 3,
                                         [(slice(0, 2), lambda a, b_: sk_t[:].rearrange("p (h d) -> p h d", h=2)[:, :, a:b_])], sk_t, 0)
                    g_dk = lambda k: hnr(it, k, g_t[:, 1152:1664], [g_t], 8, 64, 5,
                                         [(slice(0, 8), lambda a, b_: dk_t[:].rearrange("p (h d) -> p h d", h=8)[:, :, a:b_])], dk_t, 0)
                    if need_q:
                        g_mq = lambda k: hnr(it, k, xq[:, 0:768], [xq], 8, 96, 0, [(slice(0, 8), lambda a, b_: mq_t[:, :, a:b_])], mq_t, 64)
                        g_sq = lambda k: hnr(it, k, g_t[:, 0:512], [g_t], 8, 64, 2,
                                             [(slice(0, 4), lambda a, b_: sq_t[:, 0:4, a:b_]),
                                              (slice(4, 8), lambda a, b_: sq_t[:, 4:8, 64 + a:64 + b_])], sq_t, 0)
                        g_dq = lambda k: hnr(it, k, g_t[:, 640:1152], [g_t], 8, 64, 4,
                                             [(slice(0, 8, 2), lambda a, b_: dqv[:, :, 0, a:b_]),
                                              (slice(1, 8, 2), lambda a, b_: dqv[:, :, 1, 64 + a:64 + b_])], dq_t, 0)
                        interleave([g_mq(0), g_mk(1), g_sq(2)])
                        interleave([g_dq(0), g_dk(1), g_sk(2)])
                    else:
                        interleave([g_mk(0), g_dk(1), g_sk(2)])

                def back_tr(it, t):
                    need_q = t in q_set
                    s = it % 2

                    def tr_group(src_fn, nblk, rows, dst_tile, dst_slot0, srcs, use_dve):
                        p = ps_tr()
                        pb = p[:, :].bitcast(BF16)
                        for j in range(nblk):
                            K.op(PE, lambda e, j=j: e.transpose(pb[0:rows, j * 128:(j + 1) * 128], src_fn(j), identb[:]),
                                 reads=srcs + [identb], **({"writes": [p]} if j == 0 else {"partial": [p]}))
                        dst = dst_tile[0:rows, dst_slot0:dst_slot0 + nblk, :].rearrange("p a t -> p (a t)")
                        kw = {"writes": [dst_tile]} if dst_slot0 == 0 else {"partial": [dst_tile]}
                        if use_dve:
                            K.op(DVE, lambda e: e.tensor_copy(dst, pb[0:rows, 0:nblk * 128]), reads=[p], **kw)
                        else:
                            K.op(ACT, lambda e: e.activation(out=dst, in_=pb[0:rows, 0:nblk * 128], func=AF.Copy), reads=[p], **kw)

                    tsl = slice(t * 128, (t + 1) * 128)
                    if need_q:
                        tr_group(lambda j: MQ[s][:, j, :], 8, 96, o_qtm[s], 0, [MQ[s]], False)
                        K.dma("pool", QTm.t[:, :, tsl].rearrange("h d t -> d h t"), o_qtm[s][0:96, :, :], reads=[o_qtm[s]], partial=[QTm])
                        tr_group(lambda j: SQp[s][:, j, :], 8, 128, o_qts[s], 0, [SQp[s]], False)
                        K.dma("pool", QTs.t[:, :, tsl].rearrange("h d t -> d h t"), o_qts[s][:], reads=[o_qts[s]], partial=[QTs])
                        tr_group(lambda j: DQp[s][:, j, :], 8, 128, o_qtd[s], 0, [DQp[s]], False)
                        K.dma("pool", QTd.t[:, :, tsl].rearrange("h d t -> d h t"), o_qtd[s][:], reads=[o_qtd[s]], partial=[QTd])
                    tr_group(lambda j: MK[s][:, j, :], 8, 96, o_ktm[s], 0, [MK[s]], False)
                    K.dma("pool", KTm.t[:, :, tsl].rearrange("h d t -> d h t"), o_ktm[s][0:96, :, :], reads=[o_ktm[s]], partial=[KTm])
                    tr_group(lambda j: DKt[s][:, j * 128:(j + 1) * 128], 4, 128, o_kt[s], 0, [DKt[s]], False)
                    tr_group(lambda j: SKt[s][:, :], 1, 128, o_kt[s], 4, [SKt[s]], False)
                    K.dma("pool", KTd.t[:, :, tsl].rearrange("h d t -> d h t"), o_kt[s][:, 0:4, :], reads=[o_kt[s]], partial=[KTd])
                    K.dma("pool", KTs.t[:, tsl], o_kt[s][:, 4, :], reads=[o_kt[s]], partial=[KTs])

                nkv = len(kv_tiles)
                front1(0, kv_tiles[0])
                for it in range(nkv + 2):
                    if it < nkv:
                        front2(it, kv_tiles[it])
                    if it + 1 < nkv:
                        front1(it + 1, kv_tiles[it + 1])
                    if 1 <= it <= nkv:
                        back_h(it - 1, kv_tiles[it - 1])
                    if it >= 2:
                        back_tr(it - 2, kv_tiles[it - 2])
                ps_pool[0] = list(PS)

            K.barrier()
            lam_init = 0.8 - 0.6 * math.exp(-0.3 * l)
            with ExitStack() as st:
                PT = sb(st, "PT", [128, 512], BF16, 6)
                osb = sb(st, "osb", [128, 512], F32, 6)
                yst = sb(st, "yst", [128, 512], BF16, 4)
                cnt = {"s": 0, "pt": 0, "o": 0, "y": 0, "m": 0, "j": 0}
                MISC = [PS[6], PS[7]]
                pending = []

                def misc():
                    p = MISC[cnt["m"] % 2]
                    cnt["m"] += 1
                    return p

                def defer(fin):
                    while pending:
                        pending.pop(0)()
                    if fin is not None:
                        pending.append(fin)

                lamt = sb(st, "lamt", [128, 4, 64], F32)
                lsc = sb(st, "lsc", [128, 8], F32)
                K.dma("sp", lamt[:].rearrange("p a d -> p (a d)"), LAMV.t[l:l + 1, :].partition_broadcast(128), reads=[LAMV], writes=[lamt])
                K.op(DVE, lambda e: e.tensor_tensor(out=lamt[:, 0, :], in0=lamt[:, 0, :], in1=lamt[:, 1, :], op=ALU.mult), writes=[lamt])
                K.op(DVE, lambda e: e.tensor_tensor(out=lamt[:, 2, :], in0=lamt[:, 2, :], in1=lamt[:, 3, :], op=ALU.mult), writes=[lamt])
                K.op(DVE, lambda e: e.tensor_reduce(out=lsc[:, 0:1], in_=lamt[:, 0, :], axis=AX.X, op=ALU.add), reads=[lamt], writes=[lsc])
                K.op(DVE, lambda e: e.tensor_reduce(out=lsc[:, 1:2], in_=lamt[:, 2, :], axis=AX.X, op=ALU.add), reads=[lamt], writes=[lsc])
                K.op(ACT, lambda e: e.activation(out=lsc[:, 2:4], in_=lsc[:, 0:2], func=AF.Exp), writes=[lsc])
                K.op(DVE, lambda e: e.tensor_tensor(out=lsc[:, 4:5], in0=lsc[:, 3:4], in1=lsc[:, 2:3], op=ALU.subtract), writes=[lsc])
                K.op(DVE, lambda e: e.tensor_scalar(lsc[:, 5:6], lsc[:, 4:5], -lam_init, None, ALU.add), writes=[lsc])
                neglam = lsc[:, 5:6]
                gcs = sb(st, "gcs", [128, 24], F32)
                K.dma("sp", gcs[:], GCOL.t[l], reads=[GCOL], writes=[gcs])
                K.op(DVE, lambda e: e.tensor_scalar(gcs[:, 21:22], gcs[:, 21:22], 1.0 - lam_init, None, ALU.mult), writes=[gcs])
                sk8 = sb(st, "sk8", [128, 8], F32)
                sinkrow = sb(st, "sinkrow", [128, 8, 128], F32)
                K.dma("sp", sk8[:], SINK.t[l:l + 1, :].partition_broadcast(128), reads=[SINK], writes=[sk8])
                K.op(ACT, lambda e: e.activation(out=sk8[:], in_=sk8[:], func=AF.Exp), writes=[sk8])
                K.op(DVE, lambda e: e.tensor_copy(sinkrow[:], sk8[:].unsqueeze(2).to_broadcast([128, 8, 128])), reads=[sk8], writes=[sinkrow])

                def run_job(maps, kts, n, scale, mask_fn=None):
                    nk = len(kts)
                    lag = 2 if len(maps) == 1 else 1
                    hist = []
                    for i in range(nk + lag):
                        cur = None
                        if i < nk:
                            kt = kts[i]
                            cur = []
                            for m in maps:
                                sbank = PS[cnt["s"] % 3]
                                cnt["s"] += 1
                                K.op(PE, lambda e, m=m, kt=kt, sbank=sbank: e.matmul(sbank[:, 0:n], m["kt_fn"](kt), m["q_ap"], start=True, stop=True),
                                     reads=m["kt_tiles"] + m["q_tiles"], writes=[sbank])
                                pt = PT[cnt["pt"] % 6]
                                cnt["pt"] += 1
                                K.op(ACT, lambda e, pt=pt, sbank=sbank: e.activation(out=pt[:, 0:n], in_=sbank[:, 0:n], func=AF.Exp, scale=scale),
                                     reads=[sbank], writes=[pt])
                                mi = mask_fn(kt) if mask_fn else None
                                if mi is not None:
                                    K.op(DVE, lambda e, pt=pt, mi=mi: e.tensor_tensor(
                                        out=pt[:, 0:n].rearrange("p (h t) -> p h t", t=128), in0=pt[:, 0:n].rearrange("p (h t) -> p h t", t=128),
                                        in1=masks[:, mi, :].unsqueeze(1).to_broadcast([128, n // 128, 128]), op=ALU.mult),
                                        reads=[masks], writes=[pt])
                                for (AE, acc_t, c0, c1) in m.get("acc", ()):
                                    if i == 0:
                                        K.op(AE, lambda e, pt=pt, acc_t=acc_t, c0=c0, c1=c1: e.tensor_copy(acc_t[:, 0:c1 - c0], pt[:, c0:c1]), reads=[pt], writes=[acc_t])
                                    else:
                                        K.op(AE, lambda e, pt=pt, acc_t=acc_t, c0=c0, c1=c1: e.tensor_tensor(out=acc_t[:, 0:c1 - c0], in0=acc_t[:, 0:c1 - c0], in1=pt[:, c0:c1], op=ALU.add),
                                             reads=[pt], writes=[acc_t])
                                cur.append(pt)
                            hist.append(cur)
                        if i >= lag:
                            ip = i - lag
                            ktp = kts[ip]
                            for m, pt in zip(maps, hist[ip]):
                                for (v_fn, v_tiles, ob, rows) in m["pvs"]:
                                    K.op(PE, lambda e, v_fn=v_fn, ob=ob, pt=pt, ktp=ktp, rows=rows: e.matmul(ob[0:rows, 0:n], v_fn(ktp), pt[:, 0:n],
                                                                                                            start=(ip == 0), stop=(ip == nk - 1)),
                                         reads=[pt] + v_tiles, **({"writes": [ob]} if ip == 0 else {"partial": [ob]}))

                def evac(ob, n, rows=65):
                    o_s = osb[cnt["o"] % 6]
                    cnt["o"] += 1
                    K.op(ACT, lambda e: e.activation(out=o_s[0:rows, 0:n], in_=ob[0:rows, 0:n], func=AF.Copy), reads=[ob], writes=[o_s])
                    return o_s

                def finish_simple(ob, n, sink_g, store_fn):
                    o_s = evac(ob, n)

                    def fin():
                        if sink_g is not None:
                            K.op(DVE, lambda e: e.tensor_tensor(out=o_s[64:65, 0:n], in0=o_s[64:65, 0:n],
                                                                in1=sinkrow[64:65, sink_g * 4:(sink_g + 1) * 4, :].rearrange("p h t -> p (h t)"), op=ALU.add),
                                 reads=[sinkrow], writes=[o_s])
                        K.op(DVE, lambda e: e.reciprocal(o_s[64:65, 0:n], o_s[64:65, 0:n]), writes=[o_s])
                        bc = misc()
                        K.op(PE, lambda e: e.matmul(bc[:, 0:n], selF[0:65, :], o_s[0:65, 0:n], start=True, stop=True), reads=[selF, o_s], writes=[bc])
                        y = yst[cnt["y"] % 4]
                        cnt["y"] += 1
                        K.op(DVE, lambda e: e.tensor_tensor(out=y[0:64, 0:n], in0=o_s[0:64, 0:n], in1=bc[0:64, 0:n], op=ALU.mult), reads=[o_s, bc], writes=[y])
                        store_fn(y)

                    return fin

                with ExitStack() as st2:
                    Vx = sb(st2, "Vx", [128, TT, 8, 65], BF16)
                    KTt = sb(st2, "KTt", [96, T], BF16, 2)
                    Qt = sb(st2, "Qt", [96, 512], BF16, 3)
                    K.op(POOL, lambda e: e.memset(Vx[:], 1.0), writes=[Vx])
                    for k0 in range(TT):
                        K.dma("sp", Vx[:, k0, :, 0:64], Vm.t[k0 * 128:(k0 + 1) * 128, :].rearrange("p (h d) -> p h d", h=8),
                              reads=[Vm], partial=[Vx])
                    for h in range(8):
                        kt_t = KTt[h % 2]
                        K.dma("sp", kt_t[:], KTm.t[h], reads=[KTm], writes=[kt_t])
                        for ch in chunks:
                            q0, n = ch[0] * 128, len(ch) * 128
                            q_t = Qt[cnt["j"] % 3]
                            ob = PS[3 + cnt["j"] % 2]
                            cnt["j"] += 1
                            K.dma("sp", q_t[:, 0:n], QTm.t[h, :, q0:q0 + n], reads=[QTm], writes=[q_t])
                            kts = [0, 1] if ch[0] < NCTX_T else kv_tiles
                            run_job([dict(kt_fn=lambda kt, kt_t=kt_t: kt_t[:, kt * 128:(kt + 1) * 128], kt_tiles=[kt_t], q_ap=q_t[:, 0:n], q_tiles=[q_t],
                                          pvs=[(lambda kt, h=h: Vx[:, kt, h, :], [Vx], ob, 65)])], kts, n, MLA_SCALE)
                            defer(finish_simple(ob, n, None,
                                                lambda y, h=h, q0=q0, n=n: K.dma("pool", YT.t[h * 64:(h + 1) * 64, q0:q0 + n], y[0:64, 0:n], reads=[y], partial=[YT])))
                    defer(None)

                K.barrier()
                with ExitStack() as st2:
                    Vx = sb(st2, "Vxs", [128, TT, 2, 65], BF16)
                    KTt = sb(st2, "KTst", [128, T], BF16)
                    Qs = sb(st2, "Qs", [128, 8, 128], BF16, 3)
                    K.op(POOL, lambda e: e.memset(Vx[:], 1.0), writes=[Vx])
                    for k0 in range(TT):
                        K.dma("sp", Vx[:, k0, :, 0:64], Vs.t[k0 * 128:(k0 + 1) * 128, :].rearrange("p (h d) -> p h d", h=2),
                              reads=[Vs], partial=[Vx])
                    K.dma("sp", KTt[:], KTs.t[:, :], reads=[KTs], writes=[KTt])
                    for qi, t in enumerate(q_tiles):
                        q_t = Qs[qi % 3]
                        tsl = slice(t * 128, (t + 1) * 128)
                        K.dma("sp", q_t[:], QTs.t[:, :, tsl].rearrange("h p t -> p h t"), reads=[QTs], writes=[q_t])
                        if t < NCTX_T:
                            kts = [0, 1]
                            mk = {}
                        else:
                            j = t - NCTX_T
                            pv_t = NCTX_T + (j - 1) % NLT
                            nx_t = NCTX_T + (j + 1) % NLT
                            kts = [0, 1, pv_t, t, nx_t]
                            mk = {pv_t: swa_mask(j, 0), nx_t: swa_mask(j, 1)}
                        run_job([dict(kt_fn=lambda kt: KTt[:, kt * 128:(kt + 1) * 128], kt_tiles=[KTt],
                                      q_ap=q_t[:, 4 * g:4 * g + 4, :], q_tiles=[q_t],
                                      pvs=[(lambda kt, g=g: Vx[:, kt, g, :], [Vx], PS[3 + g], 65)]) for g in range(2)], kts, 512, HD_SCALE,
                                mask_fn=(lambda kt, mk=mk: mk.get(kt)))
                        fins = [finish_simple(PS[3 + g], 512, g,
                                              lambda y, g=g, tsl=tsl: K.dma("pool", YT.t[512 + g * 256:512 + (g + 1) * 256, tsl].rearrange("(h d) t -> d h t", d=64),
                                                                          y[0:64, 0:512].rearrange("d (h t) -> d h t", h=4), reads=[y], partial=[YT]))
                                for g in range(2)]
                        defer(lambda fins=fins: [f() for f in fins])
                    defer(None)

                K.barrier()
                with ExitStack() as st2:
                    Vx = sb(st2, "Vxd", [128, TT, 128], BF16, 2)
                    KTt = sb(st2, "KTdt", [128, T], BF16, 2)
                    Qd = sb(st2, "Qd", [128, 2, 512], BF16, 3)
                    accD = sb(st2, "accD", [128, 256], F32, 2)
                    accP = sb(st2, "accP", [128, 256], F32, 2)
                    rr = sb(st2, "rrd", [128, 512], F32, 2)
                    sqd = sb(st2, "sqd", [128, 512], F32)
                    lnv = sb(st2, "lnv", [128, 512], F32)
                    onesF = sb(st2, "onesF", [128, 128], F32)
                    onesB = sb(st2, "onesB", [128, 128], BF16)
                    K.op(DVE, lambda e: e.memset(onesF[:], 1.0), writes=[onesF])
                    K.op(DVE, lambda e: e.memset(onesB[:], 1.0), writes=[onesB])
                    jd = 0
                    for h in range(4):
                        vx, kt_t = Vx[h % 2], KTt[h % 2]
                        first = True
                        for k0 in range(0, TT, 16):
                            k1 = min(TT, k0 + 16)
                            K.dma("sp", vx[:, k0:k1, :], Vd.t[k0 * 128:k1 * 128, h * 128:(h + 1) * 128].rearrange("(k p) d -> p k d", p=128),
                                  reads=[Vd], **({"writes": [vx]} if first else {"partial": [vx]}))
                            first = False
                        K.dma("sp", kt_t[:], KTd.t[h], reads=[KTd], writes=[kt_t])
                        for ch in chunks:
                            q0, n = ch[0] * 128, len(ch) * 128
                            q_t = Qd[cnt["j"] % 3]
                            cnt["j"] += 1
                            K.dma("sp", q_t[:, :, 0:n], QTd.t[2 * h:2 * h + 2, :, q0:q0 + n].rearrange("m p t -> p m t"), reads=[QTd], writes=[q_t])
                            kts = [0, 1] if ch[0] < NCTX_T else kv_tiles
                            aD, aP = accD[jd % 2], accP[jd % 2]
                            jd += 1
                            nh = n // 2
                            maps = [
                                dict(kt_fn=lambda kt, kt_t=kt_t: kt_t[:, kt * 128:(kt + 1) * 128], kt_tiles=[kt_t],
                                     q_ap=q_t[:, 0, 0:n], q_tiles=[q_t], acc=[(DVE, aD, 0, nh), (POOL, aP, nh, n)],
                                     pvs=[(lambda kt, vx=vx: vx[:, kt, :], [vx], PS[3], 128)]),
                                dict(kt_fn=lambda kt, kt_t=kt_t: kt_t[:, kt * 128:(kt + 1) * 128], kt_tiles=[kt_t],
                                     q_ap=q_t[:, 1, 0:n], q_tiles=[q_t],
                                     pvs=[(lambda kt, vx=vx: vx[:, kt, :], [vx], PS[4], 128),
                                          (lambda kt: onesB[:, :], [onesB], PS[5], 128)]),
                            ]
                            run_job(maps, kts, n, HD_SCALE)
                            o1, o2, s2 = evac(PS[3], n, 128), evac(PS[4], n, 128), evac(PS[5], n, 128)

                            def fin(o1=o1, o2=o2, s2=s2, aD=aD, aP=aP, n=n, nh=nh, h=h, q0=q0):
                                s1 = misc()
                                K.op(PE, lambda e: e.matmul(s1[:, 0:nh], onesF[:, :], aD[:, 0:nh], start=True, stop=True), reads=[onesF, aD], writes=[s1])
                                K.op(PE, lambda e: e.matmul(s1[:, nh:n], onesF[:, :], aP[:, 0:nh], start=True, stop=True), reads=[onesF, aP], partial=[s1])
                                K.op(DVE, lambda e: e.reciprocal(rr[0][:, 0:n], s1[:, 0:n]), reads=[s1], writes=[rr[0]])
                                K.op(DVE, lambda e: e.reciprocal(rr[1][:, 0:n], s2[:, 0:n]), reads=[s2], writes=[rr[1]])
                                K.op(DVE, lambda e: e.tensor_scalar(rr[1][:, 0:n], rr[1][:, 0:n], neglam, None, ALU.mult), reads=[lsc], writes=[rr[1]])
                                K.op(DVE, lambda e: e.tensor_tensor(out=o1[:, 0:n], in0=o1[:, 0:n], in1=rr[0][:, 0:n], op=ALU.mult), reads=[rr[0]], writes=[o1])
                                K.op(DVE, lambda e: e.tensor_tensor(out=o2[:, 0:n], in0=o2[:, 0:n], in1=rr[1][:, 0:n], op=ALU.mult), reads=[rr[1]], writes=[o2])
                                K.op(DVE, lambda e: e.tensor_tensor(out=o1[:, 0:n], in0=o1[:, 0:n], in1=o2[:, 0:n], op=ALU.add), reads=[o2], writes=[o1])
                                K.op(ACT, lambda e: e.activation(out=sqd[:, 0:n], in_=o1[:, 0:n], func=AF.Square), reads=[o1], writes=[sqd])
                                ssb = misc()
                                K.op(PE, lambda e: e.matmul(ssb[:, 0:n], onesF[:, :], sqd[:, 0:n], start=True, stop=True), reads=[onesF, sqd], writes=[ssb])
                                K.op(ACT, lambda e: e.activation(out=lnv[:, 0:n], in_=ssb[:, 0:n], func=AF.Ln, scale=1.0 / 128, bias=EPS), reads=[ssb], writes=[lnv])
                                K.op(ACT, lambda e: e.activation(out=lnv[:, 0:n], in_=lnv[:, 0:n], func=AF.Exp, scale=-0.5), writes=[lnv])
                                y = yst[cnt["y"] % 4]
                                cnt["y"] += 1
                                K.op(DVE, lambda e: e.scalar_tensor_tensor(out=y[:, 0:n], in0=o1[:, 0:n], scalar=gcs[:, 21:22],
                                                                           in1=lnv[:, 0:n], op0=ALU.mult, op1=ALU.mult),
                                     reads=[o1, gcs, lnv], writes=[y])
                                r0 = 1024 + h * 128
                                K.dma("pool", YT.t[r0:r0 + 128, q0:q0 + n], y[:, 0:n], reads=[y], partial=[YT])

                            defer(fin)
                    defer(None)

            K.barrier()
            with ExitStack() as st:
                Wg = sb(st, "Wg", [128, 8, 3072], BF16)
                for kc in range(8):
                    K.dma("pool", Wg[:, kc, :], W_IN.t[l, kc * 128:(kc + 1) * 128, N_TM:IN_COLS], reads=[W_IN],
                          **({"writes": [Wg]} if kc == 0 else {"partial": [Wg]}))
                Wup = sb(st, "Wup", [128, 12, D], BF16)
                for b in range(3):
                    K.dma("pool", Wup[:, b * 4:(b + 1) * 4, :], W_UP.t[l, b].rearrange("(c p) n -> p c n", p=128), reads=[W_UP],
                          **({"writes": [Wup]} if b == 0 else {"partial": [Wup]}))
                Wo = sb(st, "Wo", [128, 8, D], BF16)
                K.dma("pool", Wo[:], W_O.t[l].rearrange("(c p) n -> p c n", p=128), reads=[W_O], writes=[Wo])
                G1 = sb(st, "G1", [128, D], F32)
                hTc = sb(st, "hTc", [128, 8, 512], BF16, 2)
                YTc = sb(st, "YTc", [128, 12, 512], BF16, 2)
                sig = sb(st, "sig", [128, 512], F32, 3)
                acc = sb(st, "acc", [128, 512], F32, 2)
                mT = sb(st, "mT", [128, 8, 512], BF16)
                xt1 = sb(st, "xt1", [128, D], F32, 2)
                xo1 = sb(st, "xo1", [128, D], F32, 2)
                cur_ver = None
                xi = 0
                for ci, ch in enumerate(chunks):
                    ver = 1 if ch[0] < NCTX_T else 0
                    if ver != cur_ver:
                        K.dma("sp", G1[:], MODR.t[l, ver:ver + 1, 2 * D:3 * D].partition_broadcast(128), reads=[MODR], writes=[G1])
                        cur_ver = ver
                    q0, n = ch[0] * 128, len(ch) * 128
                    h_c, y_c = hTc[ci % 2], YTc[ci % 2]
                    K.dma("sp", h_c[:, :, 0:n], HT.t[:, :, q0:q0 + n].rearrange("k p t -> p k t"), reads=[HT], writes=[h_c])
                    K.dma("sp", y_c[:, :, 0:n], YT.t[:, q0:q0 + n].rearrange("(c p) t -> p c t", p=128), reads=[YT], writes=[y_c])
                    for fc in range(8):
                        for b in range(3):
                            pg = ps()
                            for kc in range(8):
                                K.op(PE, lambda e, kc=kc, pg=pg, b=b: e.matmul(pg[:, 0:n], Wg[:, kc, b * D + fc * 128:b * D + (fc + 1) * 128], h_c[:, kc, 0:n],
                                                                              start=(kc == 0), stop=(kc == 7)),
                                     reads=[Wg, h_c], **({"writes": [pg]} if kc == 0 else {"partial": [pg]}))
                            pu = ps()
                            for c in range(4):
                                K.op(PE, lambda e, c=c, pu=pu, b=b: e.matmul(pu[:, 0:n], Wup[:, b * 4 + c, fc * 128:(fc + 1) * 128], y_c[:, b * 4 + c, 0:n],
                                                                            start=(c == 0), stop=(c == 3)),
                                     reads=[Wup, y_c], **({"writes": [pu]} if c == 0 else {"partial": [pu]}))
                            sg = sig[b]
                            K.op(ACT, lambda e, sg=sg, pg=pg: e.activation(out=sg[:, 0:n], in_=pg[:, 0:n], func=AF.Sigmoid), reads=[pg], writes=[sg])
                            if b == 0:
                                K.op(DVE, lambda e, sg=sg, pu=pu: e.tensor_tensor(out=acc[0][:, 0:n], in0=sg[:, 0:n], in1=pu[:, 0:n], op=ALU.mult),
                                     reads=[sg, pu], writes=[acc[0]])
                            else:
                                K.op(DVE, lambda e, sg=sg, pu=pu: e.tensor_tensor(out=acc[1][:, 0:n], in0=sg[:, 0:n], in1=pu[:, 0:n], op=ALU.mult),
                                     reads=[sg, pu], writes=[acc[1]])
                                if b == 1:
                                    K.op(DVE, lambda e: e.tensor_tensor(out=acc[0][:, 0:n], in0=acc[0][:, 0:n], in1=acc[1][:, 0:n], op=ALU.add),
                                         reads=[acc[1]], writes=[acc[0]])
                                else:
                                    K.op(DVE, lambda e, fc=fc: e.tensor_tensor(out=mT[:, fc, 0:n], in0=acc[0][:, 0:n], in1=acc[1][:, 0:n], op=ALU.add),
                                         reads=[acc[0], acc[1]], **({"writes": [mT]} if fc == 0 else {"partial": [mT]}))
                    for j, t in enumerate(ch):
                        x_t, x_o = xt1[xi % 2], xo1[xi % 2]
                        xi += 1
                        K.dma("sp", x_t[:], XSRC.t[t * 128:(t + 1) * 128, :], reads=[XSRC], writes=[x_t])
                        for cb in range(2):
                            p = ps()
                            for kc in range(8):
                                K.op(PE, lambda e, kc=kc, p=p, j=j, cb=cb: e.matmul(p[:, :], mT[:, kc, j * 128:(j + 1) * 128], Wo[:, kc, cb * 512:(cb + 1) * 512],
                                                                                   start=(kc == 0), stop=(kc == 7)),
                                     reads=[mT, Wo], **({"writes": [p]} if kc == 0 else {"partial": [p]}))
                            K.op(DVE, lambda e, p=p, cb=cb: e.tensor_tensor(out=acc[0][:, :], in0=p[:, :], in1=G1[:, cb * 512:(cb + 1) * 512], op=ALU.mult),
                                 reads=[p, G1], writes=[acc[0]])
                            K.op(DVE, lambda e, cb=cb, x_t=x_t, x_o=x_o: e.tensor_tensor(out=x_o[:, cb * 512:(cb + 1) * 512], in0=acc[0][:, :],
                                                                                        in1=x_t[:, cb * 512:(cb + 1) * 512], op=ALU.add),
                                 reads=[acc[0], x_t], **({"writes": [x_o]} if cb == 0 else {"partial": [x_o]}))
                        K.dma("pool", XM.t[t * 128:(t + 1) * 128, :], x_o[:], reads=[x_o], partial=[XM])

            K.barrier()
            with ExitStack() as st:
                A2, B2, gc2 = mod_cols(st, 3, 4, 8)
                Wmi = sb(st, "Wmi", [128, 8, 4 * D], BF16)
                for kc in range(8):
                    K.dma("pool", Wmi[:, kc, :], W_MI.t[l, kc * 128:(kc + 1) * 128, :], reads=[W_MI],
                          **({"writes": [Wmi]} if kc == 0 else {"partial": [Wmi]}))
                Wmo = sb(st, "Wmo", [128, 32, D], BF16)
                for c4 in range(4):
                    K.dma("pool", Wmo[:, c4 * 8:(c4 + 1) * 8, :], W_MO.t[l, c4 * D:(c4 + 1) * D, :].rearrange("(c p) n -> p c n", p=128), reads=[W_MO],
                          **({"writes": [Wmo]} if c4 == 0 else {"partial": [Wmo]}))
                G2 = sb(st, "G2", [128, D], F32)
                xt2 = sb(st, "xt2", [128, 2, D], F32, 2)
                junk2 = sb(st, "junk2", [128, D], BF16)
                ssq2 = sb(st, "ssq2", [128, 4], F32, 2)
                xn2 = sb(st, "xn2", [128, D], F32)
                h2T = sb(st, "h2T", [128, 8, 256], BF16)
                hidT = sb(st, "hidT", [128, 32, 256], BF16)
                rr = sb(st, "rr", [128, 256], F32, 2)
                xo2 = sb(st, "xo2", [128, D], F32, 2)
                tmp2 = sb(st, "tmp2", [128, 512], F32)
                chunks2 = []
                for ch in chunks:
                    for i in range(0, len(ch), 2):
                        chunks2.append(ch[i:i + 2])
                cur_ver = None
                xi = 0
                for ci, ch in enumerate(chunks2):
                    ver = 1 if ch[0] < NCTX_T else 0
                    if ver != cur_ver:
                        K.dma("sp", G2[:], MODR.t[l, ver:ver + 1, 5 * D:6 * D].partition_broadcast(128), reads=[MODR], writes=[G2])
                        cur_ver = ver
                    n = len(ch) * 128
                    x_c = xt2[ci % 2]
                    for j, t in enumerate(ch):
                        K.dma("sp", x_c[:, j, :], XM.t[t * 128:(t + 1) * 128, :], reads=[XM], **({"writes": [x_c]} if j == 0 else {"partial": [x_c]}))
                    for j, t in enumerate(ch):
                        norm_transpose(x_c[:, j, :], x_c, junk2, ssq2[j], xn2, A2, B2, ver,
                                       lambda kc, j=j: h2T[:, kc, j * 128:(j + 1) * 128], h2T, j == 0)
                    for fc in range(32):
                        p = ps()
                        for kc in range(8):
                            K.op(PE, lambda e, kc=kc, p=p, fc=fc: e.matmul(p[:, 0:n], Wmi[:, kc, fc * 128:(fc + 1) * 128], h2T[:, kc, 0:n], start=(kc == 0), stop=(kc == 7)),
                                 reads=[Wmi, h2T], **({"writes": [p]} if kc == 0 else {"partial": [p]}))
                        r_t = rr[fc % 2]
                        K.op(ACT, lambda e, p=p, r_t=r_t: e.activation(out=r_t[:, 0:n], in_=p[:, 0:n], func=AF.Relu), reads=[p], writes=[r_t])
                        K.op(DVE, lambda e, r_t=r_t, fc=fc: e.tensor_tensor(out=hidT[:, fc, 0:n], in0=r_t[:, 0:n], in1=r_t[:, 0:n], op=ALU.mult),
                             reads=[r_t], **({"writes": [hidT]} if fc == 0 else {"partial": [hidT]}))
                    for j, t in enumerate(ch):
                        x_o = xo2[xi % 2]
                        xi += 1
                        for cb in range(2):
                            p = ps()
                            for fc in range(32):
                                K.op(PE, lambda e, fc=fc, p=p, j=j, cb=cb: e.matmul(p[:, :], hidT[:, fc, j * 128:(j + 1) * 128], Wmo[:, fc, cb * 512:(cb + 1) * 512],
                                                                                   start=(fc == 0), stop=(fc == 31)),
                                     reads=[hidT, Wmo], **({"writes": [p]} if fc == 0 else {"partial": [p]}))
                            K.op(DVE, lambda e, p=p, cb=cb: e.tensor_tensor(out=tmp2[:, :], in0=p[:, :], in1=G2[:, cb * 512:(cb + 1) * 512], op=ALU.mult),
                                 reads=[p, G2], writes=[tmp2])
                            K.op(DVE, lambda e, cb=cb, j=j, x_o=x_o, x_c=x_c: e.tensor_tensor(out=x_o[:, cb * 512:(cb + 1) * 512], in0=tmp2[:, :],
                                                                                             in1=x_c[:, j, cb * 512:(cb + 1) * 512], op=ALU.add),
                                 reads=[tmp2, x_c], **({"writes": [x_o]} if cb == 0 else {"partial": [x_o]}))
                        r = dst_rows(t)
                        K.dma("pool", XDST.t[r * 128:(r + 1) * 128, :], x_o[:], reads=[x_o], partial=[XDST])
            K.barrier()
        K.finish()
    return nc


def _rope_tables(pos):
    row = (pos // GRID_W).astype(np.float32)
    col = (pos % GRID_W).astype(np.float32)
    out = []
    for rot in (64, 32):
        half = rot // 2
        freqs = (np.float32(10000.0) ** (-np.arange(0, half, 2, dtype=np.float32) / np.float32(half))).astype(np.float32)
        ar = (row[:, None] * freqs).astype(np.float32)
        ac = (col[:, None] * freqs).astype(np.float32)
        cr, sr, cc, sc = np.cos(ar), np.sin(ar), np.cos(ac), np.sin(ac)
        out.append(np.concatenate([cr, cr, cc, cc], axis=1))
        out.append(np.concatenate([-sr, sr, -sc, sc], axis=1))
    return np.concatenate(out, axis=1).astype(np.float32)


def _swap_pairs(g, q4):
    return g.reshape(2, 2, q4)[:, ::-1, :].reshape(-1)


_PROG_CACHE = {}


def kernel(**inputs):
    inp = {k: np.asarray(v) for k, v in inputs.items()}
    x = inp["x"].astype(np.float32, copy=False)
    B, L, _ = x.shape
    NLT = L // 128
    NH = NLT // 2
    n_cores = 2 * B
    if NLT not in _PROG_CACHE:
        _PROG_CACHE[NLT] = build_program(NLT)
    nc = _PROG_CACHE[NLT]

    f32 = np.float32
    w_up = np.stack([inp["w_up_mla"], inp["w_up_swa"], inp["w_up_diff"]], axis=1).astype(f32, copy=False)
    gcol = np.zeros((DEPTH, 128, 24), f32)
    grow = np.zeros((DEPTH, 12, 96), f32)
    for l in range(DEPTH):
        gcol[l, :, 0:8] = inp["g_norm_attn"][l].reshape(8, 128).T
        gcol[l, :, 8:16] = inp["g_norm_mlp"][l].reshape(8, 128).T
        gcol[l, :, 16:18] = inp["g_q_lora"][l].reshape(2, 128).T
        gcol[l, :, 18] = inp["g_kv_lora"][l]
        gcol[l, 0:64, 19] = inp["g_diff_sub"][l][0:64]
        gcol[l, 0:64, 20] = inp["g_diff_sub"][l][64:128]
        gcol[l, :, 21] = inp["g_diff_sub"][l]
        for gi, name in enumerate(("g_mla_q", "g_mla_k")):
            g = inp[name][l]
            grow[l, 2 * gi, :] = g
            grow[l, 2 * gi + 1, 64:96] = _swap_pairs(g[64:96], 8)
        for gi, name in enumerate(("g_swa_q", "g_swa_k", "g_diff_q", "g_diff_k")):
            g = inp[name][l]
            grow[l, 4 + 2 * gi, 0:64] = g
            grow[l, 5 + 2 * gi, 0:64] = _swap_pairs(g, 16)
    lamv = np.concatenate([inp["lambda_q1"], inp["lambda_k1"], inp["lambda_q2"], inp["lambda_k2"]], axis=1).astype(f32)
    ii = np.arange(128)
    masks = np.zeros((128, 2, 128), f32)
    masks[:, 0, :] = (ii[:, None] >= ii[None, :])
    masks[:, 1, :] = (ii[:, None] <= ii[None, :])
    identf = np.eye(128, dtype=f32)
    shared = {
        "identf": identf, "masks": masks,
        "w_mod": inp["w_mod"], "b_mod": inp["b_mod"], "gcol": gcol, "grow": grow.reshape(DEPTH, 12 * 96),
        "lamv": lamv, "sink": inp["swa_sink"], "w_in": inp["w_in"], "w_uq": inp["w_uq"], "w_ukv": inp["w_ukv"],
        "w_up": w_up, "w_o": inp["w_o"], "w_mlp_in": inp["w_mlp_in"], "w_mlp_out": inp["w_mlp_out"],
    }
    shared = {k: np.ascontiguousarray(v, dtype=f32) for k, v in shared.items()}
    in_maps = []
    for c in range(n_cores):
        b, s = c // 2, c % 2
        blocks = list(range(s * NH, (s + 1) * NH)) + list(range((1 - s) * NH, (2 - s) * NH))
        tok = np.concatenate([np.arange(bk * 128, (bk + 1) * 128) for bk in blocks])
        xin = np.concatenate([inp["ctx"][b], x[b][tok]], axis=0).astype(f32, copy=False)
        rope = np.zeros((256 + L, 192), f32)
        rope[0:256, 0:64] = 1.0
        rope[0:256, 128:160] = 1.0
        rope[256:] = _rope_tables(tok)
        cT = np.zeros((128, 16), f32)
        cT[:, 0::2] = inp["c"][b].reshape(8, 128).T
        cT[:, 1::2] = inp["c_ctx"].reshape(8, 128).T
        edge = np.zeros((128, 4), f32)
        edge[:] = [0, 0, 1, 1] if s == 0 else [1, 1, 0, 0]
        m = dict(shared)
        m.update({"xin": np.ascontiguousarray(xin), "rope": rope, "cT": cT, "edge": edge})
        in_maps.append(m)
    res = run_bass_kernel_spmd(nc, in_maps, core_ids=list(range(n_cores)))
    if DEBUG:
        DEBUG_RES.append(res.results)
    out = np.empty((B, L, D), f32)
    for c in range(n_cores):
        b, s = c // 2, c % 2
        out[b, s * NH * 128:(s + 1) * NH * 128, :] = res.results[c]["out"]
    return out
```
